# Optimizing a Trainium2 kernel written in Bass

```python
import math
import jax, jax.numpy as jnp
from jax import lax
import numpy as np

D_MODEL = 1024
BATCH = 16
SEQ = 4096
DEPTH = 2

N_EVEN = (DEPTH + 1) // 2
N_ODD = DEPTH // 2

SSM_WIDTH = D_MODEL // 4
SSM_GROUP = 16
SSM_GROUPS = SSM_WIDTH // SSM_GROUP
SSM_STATE = 64
GMLP_WIDTH = D_MODEL - SSM_WIDTH
GMLP_HEAD = 128
GMLP_HEADS = GMLP_WIDTH // GMLP_HEAD
CHUNK = 128
EVEN_IN = SSM_WIDTH + 2 * GMLP_WIDTH

CONV_WIDTH = 3
D_FF = 2816
EPS = 1e-6
DT_MIN = 1e-3
DT_MAX = 1e-1
LAMBDA_RE_MAX = -1e-4
RESID_SCALE = (2 * DEPTH) ** -0.5

kernel_name = "hybrid_s5_gmlp_shortconv_convffn"


def rmsnorm(x, g):
    xf = x.astype(jnp.float32)
    y = xf * lax.rsqrt(jnp.mean(xf * xf, axis=-1, keepdims=True) + EPS)
    return (y * g.astype(jnp.float32)).astype(x.dtype)


def causal_dwconv(x, w, b):
    k_w = w.shape[0]
    s = x.shape[1]
    xp = jnp.pad(x, ((0, 0), (k_w - 1, 0), (0, 0)))
    return b + sum(w[k] * xp[:, k:k + s] for k in range(k_w))


def s5_mixer(u, lam_re, lam_im, log_dt, b_re, b_im, c_re, c_im, d_skip, w_glu, b_glu):
    bsz, s, _ = u.shape
    uf = u.astype(jnp.float32).reshape(bsz, s, SSM_GROUPS, SSM_GROUP)
    lr = jnp.minimum(lam_re.astype(jnp.float32), LAMBDA_RE_MAX)
    li = lam_im.astype(jnp.float32)
    dt = jnp.exp(log_dt.astype(jnp.float32))[:, None]
    mag = jnp.exp(lr * dt)
    ab_re = mag * jnp.cos(li * dt)
    ab_im = mag * jnp.sin(li * dt)
    den = lr * lr + li * li
    nr = ab_re - 1.0
    ni = ab_im
    z_re = ((nr * lr + ni * li) / den)[..., None]
    z_im = ((ni * lr - nr * li) / den)[..., None]
    br = b_re.astype(jnp.float32)
    bi = b_im.astype(jnp.float32)
    bb_re = z_re * br - z_im * bi
    bb_im = z_re * bi + z_im * br
    x_re = jnp.einsum('bsgh,gph->bsgp', uf, bb_re)
    x_im = jnp.einsum('bsgh,gph->bsgp', uf, bb_im)
    a_re = jnp.broadcast_to(ab_re, (1, s) + ab_re.shape)
    a_im = jnp.broadcast_to(ab_im, (1, s) + ab_im.shape)

    def combine(left, right):
        a1r, a1i, b1r, b1i = left
        a2r, a2i, b2r, b2i = right
        return (a2r * a1r - a2i * a1i,
                a2r * a1i + a2i * a1r,
                a2r * b1r - a2i * b1i + b2r,
                a2r * b1i + a2i * b1r + b2i)

    _, _, h_re, h_im = lax.associative_scan(combine, (a_re, a_im, x_re, x_im), axis=1)
    y = (jnp.einsum('bsgp,ghp->bsgh', h_re, c_re.astype(jnp.float32))
         - jnp.einsum('bsgp,ghp->bsgh', h_im, c_im.astype(jnp.float32)))
    y = (y + d_skip.astype(jnp.float32).reshape(SSM_GROUPS, SSM_GROUP) * uf).reshape(bsz, s, SSM_WIDTH)
    y = jax.nn.gelu(y)
    y = y * jax.nn.sigmoid(y @ w_glu.astype(jnp.float32) + b_glu.astype(jnp.float32))
    return y.astype(u.dtype)


def gmlp_mixer(uv, w_s, b_s, g_v):
    bsz, s, _ = uv.shape
    u, v = jnp.split(jax.nn.gelu(uv), 2, axis=-1)
    v = rmsnorm(v, g_v).reshape(bsz, s // CHUNK, CHUNK, GMLP_HEADS, GMLP_HEAD)
    mask = jnp.tril(jnp.ones((CHUNK, CHUNK), dtype=bool))
    w = jnp.where(mask, w_s, 0)
    gate = jnp.einsum('hts,bnshc->bnthc', w, v) + b_s.T[None, None, :, :, None]
    return u * gate.reshape(bsz, s, GMLP_WIDTH)


def shortconv_mixer(p, w_conv, b_conv):
    bg, cg, hx = jnp.split(p, 3, axis=-1)
    return bg * causal_dwconv(cg * hx, w_conv, b_conv)


def conv_ffn(x, w_up, w_conv, b_conv, w_down):
    h = causal_dwconv(x @ w_up, w_conv, b_conv)
    gate, val = jnp.split(h, 2, axis=-1)
    return (jax.nn.silu(gate) * val) @ w_down


def setup_inputs(seed: int = 0) -> dict:
    key = jax.random.key(seed)
    ks = iter(jax.random.split(key, 32))
    f32 = jnp.float32
    nrm = lambda shape, std: std * jax.random.normal(next(ks), shape, f32)
    d = D_MODEL
    inp = {}
    inp["x"] = nrm((BATCH, SEQ, d), 1.0)
    inp["mix_norm_g"] = 1.0 + nrm((DEPTH, d), 0.02)
    inp["ffn_norm_g"] = 1.0 + nrm((DEPTH, d), 0.02)
    inp["final_norm_g"] = 1.0 + nrm((d,), 0.02)
    inp["ev_w_in"] = nrm((N_EVEN, d, EVEN_IN), d ** -0.5)
    inp["ev_w_out"] = nrm((N_EVEN, d, d), d ** -0.5 * RESID_SCALE)
    inp["s5_lam_re"] = -0.5 + nrm((N_EVEN, SSM_GROUPS, SSM_STATE), 0.01)
    n_idx = jnp.arange(SSM_STATE, dtype=f32)
    inp["s5_lam_im"] = math.pi * n_idx + nrm((N_EVEN, SSM_GROUPS, SSM_STATE), 0.01)
    inp["s5_log_dt"] = jax.random.uniform(next(ks), (N_EVEN, SSM_GROUPS), f32,
                                          math.log(DT_MIN), math.log(DT_MAX))
    inp["s5_b_re"] = nrm((N_EVEN, SSM_GROUPS, SSM_STATE, SSM_GROUP), (2 * SSM_GROUP) ** -0.5)
    inp["s5_b_im"] = nrm((N_EVEN, SSM_GROUPS, SSM_STATE, SSM_GROUP), (2 * SSM_GROUP) ** -0.5)
    inp["s5_c_re"] = nrm((N_EVEN, SSM_GROUPS, SSM_GROUP, SSM_STATE), SSM_STATE ** -0.5)
    inp["s5_c_im"] = nrm((N_EVEN, SSM_GROUPS, SSM_GROUP, SSM_STATE), SSM_STATE ** -0.5)
    inp["s5_d"] = nrm((N_EVEN, SSM_WIDTH), 1.0)
    inp["s5_w_glu"] = nrm((N_EVEN, SSM_WIDTH, SSM_WIDTH), SSM_WIDTH ** -0.5)
    inp["s5_b_glu"] = nrm((N_EVEN, SSM_WIDTH), 0.01)
    inp["gm_w_s"] = nrm((N_EVEN, GMLP_HEADS, CHUNK, CHUNK), CHUNK ** -0.5)
    inp["gm_b_s"] = 1.0 + nrm((N_EVEN, GMLP_HEADS, CHUNK), 0.01)
    inp["gm_v_g"] = 1.0 + nrm((N_EVEN, GMLP_WIDTH), 0.02)
    inp["od_w_in"] = nrm((N_ODD, d, 3 * d), d ** -0.5)
    inp["od_conv_w"] = nrm((N_ODD, CONV_WIDTH, d), CONV_WIDTH ** -0.5)
    inp["od_conv_b"] = nrm((N_ODD, d), 0.01)
    inp["od_w_out"] = nrm((N_ODD, d, d), d ** -0.5 * RESID_SCALE)
    inp["ffn_w_up"] = nrm((DEPTH, d, 2 * D_FF), d ** -0.5)
    inp["ffn_conv_w"] = nrm((DEPTH, CONV_WIDTH, 2 * D_FF), CONV_WIDTH ** -0.5)
    inp["ffn_conv_b"] = nrm((DEPTH, 2 * D_FF), 0.01)
    inp["ffn_w_down"] = nrm((DEPTH, D_FF, d), D_FF ** -0.5 * RESID_SCALE)
    return inp


def reference(x, mix_norm_g, ffn_norm_g, final_norm_g,
              ev_w_in, ev_w_out, s5_lam_re, s5_lam_im, s5_log_dt,
              s5_b_re, s5_b_im, s5_c_re, s5_c_im, s5_d, s5_w_glu, s5_b_glu,
              gm_w_s, gm_b_s, gm_v_g,
              od_w_in, od_conv_w, od_conv_b, od_w_out,
              ffn_w_up, ffn_conv_w, ffn_conv_b, ffn_w_down):
    h = x
    for layer in range(DEPTH):
        y = rmsnorm(h, mix_norm_g[layer])
        if layer % 2 == 0:
            e = layer // 2
            p = y @ ev_w_in[e]
            a_out = s5_mixer(p[..., :SSM_WIDTH], s5_lam_re[e], s5_lam_im[e], s5_log_dt[e],
                             s5_b_re[e], s5_b_im[e], s5_c_re[e], s5_c_im[e],
                             s5_d[e], s5_w_glu[e], s5_b_glu[e])
            b_out = gmlp_mixer(p[..., SSM_WIDTH:], gm_w_s[e], gm_b_s[e], gm_v_g[e])
            mix = jnp.concatenate([a_out, b_out], axis=-1) @ ev_w_out[e]
        else:
            o = layer // 2
            mix = shortconv_mixer(y @ od_w_in[o], od_conv_w[o], od_conv_b[o]) @ od_w_out[o]
        h = h + mix
        h = h + conv_ffn(rmsnorm(h, ffn_norm_g[layer]), ffn_w_up[layer], ffn_conv_w[layer],
                         ffn_conv_b[layer], ffn_w_down[layer])
    return rmsnorm(h, final_norm_g)
```

```python
import os
import numpy as np
from contextlib import ExitStack
import concourse.bass as bass
import concourse.mybir as mybir
from concourse.bass_utils import run_bass_kernel_spmd

F32 = mybir.dt.float32
BF16 = mybir.dt.bfloat16
I32 = mybir.dt.int32
AF = mybir.ActivationFunctionType
ALU = mybir.AluOpType

NCORES = 8
D = 1024
SEQ = 4096
NSEQ = 2
SEGT = 1024
NSEGS = SEQ // SEGT
TT = 512
DFF = 2816
NPAIR = DFF // 128
EPS = 1e-6
NSLAB = 103
BS = 4
NB = SEGT // BS
PAD = 128
PIPE_FFN = os.environ.get('K_PIPE_FFN', '1') == '1'
PIPE_L1 = os.environ.get('K_PIPE_L1', '1') == '1'
NORM_IL = os.environ.get('K_NORM_IL', '1') == '1'


class Buf:
    __slots__ = ("name", "w", "r")

    def __init__(self, name):
        self.name = name
        self.w = None
        self.r = {}


class Eng:
    def __init__(self, name, eng, sem, selfwait):
        self.name = name
        self.eng = eng
        self.sem = sem
        self.count = 0
        self.seen = {}
        self.selfwait = selfwait


class Tracker:
    def __init__(self, nc, stack):
        self.nc = nc
        self.stack = stack
        self.engs = {}
        for name, eng, sw in (("pe", nc.tensor, False), ("act", nc.scalar, True),
                              ("dve", nc.vector, True), ("pool", nc.gpsimd, True),
                              ("sp", nc.sync, True)):
            sem = stack.enter_context(nc.semaphore("sem_" + name))
            self.engs[name] = Eng(name, eng, sem, sw)
        self.dma_sems = {}

    def dma_sem(self, key):
        if key not in self.dma_sems:
            sem = self.stack.enter_context(self.nc.semaphore("dq_" + str(key)))
            self.dma_sems[key] = [sem, 0]
        return self.dma_sems[key]

    def _wait(self, E, deps):
        best = {}
        for d in deps:
            if d is None:
                continue
            k, v = d
            if best.get(k, 0) < v:
                best[k] = v
        for k, v in best.items():
            if (not E.selfwait) and k is E.sem:
                continue
            if E.seen.get(k, 0) < v:
                E.eng.wait_ge(k, v)
                E.seen[k] = v

    @staticmethod
    def _flat(bufs):
        out = []
        for b in bufs:
            if isinstance(b, (list, tuple)):
                out.extend(Tracker._flat(b))
            else:
                out.append(b)
        return out

    @staticmethod
    def _deps(reads, writes):
        reads = Tracker._flat(reads)
        writes = Tracker._flat(writes)
        deps = []
        for b in reads:
            deps.append(b.w)
        for b in writes:
            deps.append(b.w)
            for k, v in b.r.items():
                deps.append((k, v))
        return deps

    @staticmethod
    def _commit(tok, reads, writes):
        reads = Tracker._flat(reads)
        writes = Tracker._flat(writes)
        k, v = tok
        for b in reads:
            if b.r.get(k, 0) < v:
                b.r[k] = v
        for b in writes:
            b.w = tok
            b.r = {}

    def op(self, ename, fn, reads=(), writes=()):
        E = self.engs[ename]
        self._wait(E, self._deps(reads, writes))
        inst = fn(E.eng)
        E.count += 1
        inst.then_inc(E.sem, 1)
        self._commit((E.sem, E.count), reads, writes)
        return inst

    def group(self, ename, fns, reads=(), writes=()):
        E = self.engs[ename]
        self._wait(E, self._deps(reads, writes))
        inst = None
        for fn in fns:
            inst = fn(E.eng)
        E.count += 1
        inst.then_inc(E.sem, 1)
        self._commit((E.sem, E.count), reads, writes)
        return inst

    def dma(self, key, fn, reads=(), writes=(), ename="sp"):
        E = self.engs[ename]
        ent = self.dma_sem(key)
        self._wait(E, self._deps(reads, writes) + [(ent[0], ent[1])])
        inst = fn(E.eng)
        ent[1] += 16
        inst.then_inc(ent[0], 16)
        self._commit((ent[0], ent[1]), reads, writes)
        return inst

    def barrier(self, final=False):
        toks = [(E.sem, E.count) for E in self.engs.values() if E.count > 0]
        toks += [(s, v) for key, (s, v) in self.dma_sems.items() if v > 0 and (final or not str(key).startswith("wc"))]
        for E in self.engs.values():
            self._wait(E, toks)


def _small_layout():
    off = {}
    n = 0
    for name, w in (("g_mix", 16), ("g_ffn", 16), ("g_fin", 8), ("lamre", 8), ("lamim", 8),
                    ("logdt", 8), ("Bre", 128), ("Bim", 128), ("Cre", 128), ("Cim", 128),
                    ("Dcol", 2), ("bglu", 2), ("cw_ffn", 2 * 44 * 3), ("cb_ffn", 2 * 44),
                    ("cw_od", 24), ("cb_od", 8)):
        off[name] = n
        n += w
    return off, n


SOFF, NSMALL = _small_layout()
MOFF = {"gvg": 0, "wsT": 768, "wglu": 1536, "ident": 2048, "mask": 2176}
NMED = 2304


def build_program(nseg_total=NSEQ * NSEGS, stop_after=None):
    nc = bass.Bass("TRN2", target_bir_lowering=False)
    x_d = nc.dram_tensor("x", [NSEQ, SEQ, D], F32, kind="ExternalInput").ap()
    wsl_d = nc.dram_tensor("wsl", [NSLAB, 128, 2048], F32, kind="ExternalInput").ap()
    small_d = nc.dram_tensor("smallp", [128, NSMALL], F32, kind="ExternalInput").ap()
    med_d = nc.dram_tensor("medp", [128, NMED], F32, kind="ExternalInput").ap()
    bs_d = nc.dram_tensor("bsrow", [1, 768], F32, kind="ExternalInput").ap()
    out_d = nc.dram_tensor("out", [NSEQ, SEQ, D], F32, kind="ExternalOutput").ap()
    wbf_d = nc.dram_tensor("wbf_scratch", [NSLAB, 128, 2048], BF16, kind="Internal").ap()

    with ExitStack() as st:
        tr = Tracker(nc, st)

        uniq = [0]

        def sb(name, shape, dt, stack=st):
            uniq[0] += 1
            return stack.enter_context(nc.sbuf_tensor("%s_%d" % (name, uniq[0]), shape, dt))

        smallp = sb("smallp_s", [128, NSMALL], F32)
        Wst = sb("Wst", [128, 64, 128], BF16)
        Wout = sb("Wout", [128, 64, 128], BF16)
        Kbd = sb("Kbd", [128, 8, 128], BF16)
        aRe = sb("aRe", [128, 8, 8], F32)
        aIm = sb("aIm", [128, 8, 8], F32)
        naIm = sb("naIm", [128, 8, 8], F32)
        carry = sb("carry", [128, 8, 2], F32)
        gvg = sb("gvg", [128, 768], F32)
        WsT = sb("WsT", [128, 6, 128], BF16)
        wglu = sb("wglu", [128, 2, 256], BF16)
        ident = sb("ident", [128, 128], F32)
        onesM = sb("onesM", [128, 128], BF16)
        ones2 = sb("ones2", [33, 128], BF16)
        bs2 = sb("bs2", [33, 768], BF16)
        epsc = sb("epsc", [128, 1], F32)
        dummy = sb("dummy", [128, 2], F32)
        B_dummy = Buf("dummy")
        halo_f = sb("halo_f", [128, 2, 2, 44, 2], F32)
        halo_q = sb("halo_q", [128, 2, 8, 2], F32)
        B_small = Buf("small")
        B_wbf = [Buf("wbf%d" % i) for i in range(NSLAB)]
        ncast = [0]

        def emit_cast(li):
            assert li == ncast[0]
            tr.dma("wc%d" % (li % 3), lambda e: e.dma_start(out=wbf_d[li], in_=wsl_d[li]), writes=[B_wbf[li]], ename="pool")
            ncast[0] += 1
        B_tab = Buf("tab")
        B_carry = Buf("carry")
        B_halo_f = [[[Buf("hf%d_%d_%d" % (pp, l, m)) for m in range(44)] for l in range(2)] for pp in range(2)]
        B_halo_q = [[Buf("hq%d_%d" % (pp, j)) for j in range(8)] for pp in range(2)]

        ps_all = st.enter_context(nc.psum_tensor("ps_all", [128, 4096], F32))
        psum = [ps_all[:, i * 512:(i + 1) * 512] for i in range(8)]
        B_ps = [Buf("ps%d" % i) for i in range(8)]
        bank_ctr = [0]

        def bank():
            i = bank_ctr[0] % 8
            bank_ctr[0] += 1
            return psum[i], B_ps[i]

        def bank2():
            if bank_ctr[0] % 2:
                bank_ctr[0] += 1
            i = bank_ctr[0] % 8
            bank_ctr[0] += 2
            return ps_all[:, i * 512:(i + 2) * 512], [psum[i], psum[i + 1]], [B_ps[i], B_ps[i + 1]]

        st.enter_context(nc.Block())

        def sm(name, j=0, w=1):
            o = SOFF[name] + j
            return smallp[:, o:o + w]

        def mm(out_ap, lhsT, rhs, start, stop):
            return lambda e: e.matmul(out_ap, lhsT=lhsT, rhs=rhs, start=start, stop=stop)

        def mm_group(out_ap, pairs, reads, writes):
            n = len(pairs)
            fns = [mm(out_ap, l, r, i == 0, i == n - 1) for i, (l, r) in enumerate(pairs)]
            tr.group("pe", fns, reads=reads, writes=writes)

        def mm_group_y(out_ap, wfn, tt, reads, writes, perk=False):
            if not perk:
                mm_group(out_ap, [(wfn(k), y[:, k, tl(tt)]) for k in range(8)], reads=reads + [B_y[tt]], writes=writes)
                return
            for k in range(8):
                tr.group("pe", [mm(out_ap, wfn(k), y[:, k, tl(tt)], k == 0, k == 7)], reads=reads + Byj(k, tt), writes=writes)

        def stt(out, in0, scalar, in1, op0=ALU.mult, op1=ALU.add):
            return lambda e: e.scalar_tensor_tensor(out=out, in0=in0, scalar=scalar, in1=in1, op0=op0, op1=op1)

        def tt_(out, in0, in1, op):
            return lambda e: e.tensor_tensor(out=out, in0=in0, in1=in1, op=op)

        def ts_(out, in0, s1, s2, op0, op1=None):
            if op1 is None:
                return lambda e: e.tensor_scalar(out=out, in0=in0, scalar1=s1, scalar2=None, op0=op0)
            return lambda e: e.tensor_scalar(out=out, in0=in0, scalar1=s1, scalar2=s2, op0=op0, op1=op1)

        def act_(out, in_, func, bias=None, scale=None):
            kw = {}
            if bias is not None:
                kw["bias"] = bias
            if scale is not None:
                kw["scale"] = scale
            return lambda e: e.activation(out=out, in_=in_, func=func, **kw)

        def cp_(out, in_):
            return lambda e: e.tensor_copy(out=out, in_=in_)

        def acp_(out, in_):
            return lambda e: e.copy(out=out, in_=in_)

        tr.dma("small", lambda e: e.dma_start(out=smallp[:], in_=small_d[:, :]), writes=[B_small])

        tr.op("pool", lambda e: e.memset(onesM[:], 1.0 / 1024.0), writes=[B_tab])
        tr.op("pool", lambda e: e.memset(ones2[:], 0.0), writes=[B_tab])
        tr.op("pool", lambda e: e.memset(bs2[:], 0.0), writes=[B_tab])
        tr.op("pool", lambda e: e.memset(ones2[0:1, :], 1.0), writes=[B_tab])
        tr.op("pool", lambda e: e.memset(ones2[32:33, :], 1.0), writes=[B_tab])
        tr.op("pool", lambda e: e.memset(epsc[:], EPS), writes=[B_tab])
        tr.op("pool", lambda e: e.memset(Wout[:], 0.0), writes=[B_tab])

        with ExitStack() as ps:
            medp = sb("medp_s", [128, NMED], F32, ps)
            B_med = Buf("med")
            bsf = sb("bsf", [33, 768], F32, ps)
            bsh = sb("bsh", [33, 768], F32, ps)
            B_bsf = Buf("bsf")
            tr.dma("bsf0", lambda e: e.dma_start(out=bsf[0:1, :], in_=bs_d[:, :]), writes=[B_bsf])
            tr.dma("bsf1", lambda e: e.dma_start(out=bsf[32:33, :], in_=bs_d[:, :]), writes=[B_bsf])
            tr.op("dve", cp_(bs2[0:1, :], bsf[0:1, :]), reads=[B_bsf], writes=[B_tab])
            tr.op("dve", cp_(bsh[32:33, :], bsf[32:33, :]), reads=[B_bsf], writes=[B_bsf])
            tr.op("dve", cp_(bs2[32:33, :], bsf[32:33, :]), reads=[B_bsf], writes=[B_tab])
            tr.op("dve", tt_(bsh[32:33, :], bsf[32:33, :], bs2[32:33, :], ALU.subtract), reads=[B_bsf, B_tab], writes=[B_bsf])
            tr.op("dve", cp_(bs2[32:33, :], bsh[32:33, :]), reads=[B_bsf], writes=[B_tab])
            tr.dma("med", lambda e: e.dma_start(out=medp[:], in_=med_d[:, :]), writes=[B_med])
            tr.op("act", acp_(gvg[:], medp[:, 0:768]), reads=[B_med], writes=[B_tab])
            tr.op("act", acp_(ident[:], medp[:, 2048:2176]), reads=[B_med], writes=[B_tab])
            tr.op("act", acp_(wglu[:].rearrange("p k c -> p (k c)"), medp[:, 1536:2048]), reads=[B_med], writes=[B_tab])
            for hd in range(6):
                tr.op("dve", tt_(WsT[:, hd, :], medp[:, 768 + hd * 128:768 + (hd + 1) * 128],
                                 medp[:, 2176:2304], ALU.mult), reads=[B_med], writes=[B_tab])

            def t8(name, w=8):
                return sb(name, [128, w], F32, ps)
            B_p = Buf("s5p")

            def P_(eng, fn):
                tr.op(eng, fn, reads=[B_p, B_small], writes=[B_p])

            lr = t8("lr"); dt_ = t8("dt"); lrdt = t8("lrdt"); th = t8("th"); mag = t8("mag")
            cs = t8("cs"); sn = t8("sn"); tmp = t8("tmp"); tmp2 = t8("tmp2"); ki = sb("ki", [128, 8], I32, ps)
            P_("dve", ts_(lr[:], sm("lamre", 0, 8), -1e-4, None, ALU.min))
            P_("act", act_(dt_[:], sm("logdt", 0, 8), AF.Exp))
            P_("dve", tt_(lrdt[:], lr[:], dt_[:], ALU.mult))
            P_("dve", tt_(th[:], sm("lamim", 0, 8), dt_[:], ALU.mult))
            P_("act", act_(mag[:], lrdt[:], AF.Exp))

            def sin_of(dst, shift):
                P_("dve", ts_(tmp[:], th[:], shift, 1.0 / (2 * np.pi), ALU.add, ALU.mult))
                P_("dve", cp_(ki[:], tmp[:]))
                P_("dve", cp_(tmp[:], ki[:]))
                P_("dve", ts_(tmp2[:], th[:], shift, None, ALU.add))
                P_("dve", stt(tmp2[:], tmp[:], -2 * np.pi, tmp2[:]))
                P_("dve", ts_(tmp[:], tmp2[:], float(np.pi), -2 * np.pi, ALU.is_gt, ALU.mult))
                P_("dve", tt_(tmp2[:], tmp2[:], tmp[:], ALU.add))
                P_("dve", ts_(tmp[:], tmp2[:], -float(np.pi), 2 * np.pi, ALU.is_lt, ALU.mult))
                P_("dve", tt_(tmp2[:], tmp2[:], tmp[:], ALU.add))
                P_("dve", ts_(tmp2[:], tmp2[:], 3.1415925, -3.1415925, ALU.min, ALU.max))
                P_("act", act_(dst[:], tmp2[:], AF.Sin))

            sin_of(sn, 0.0)
            sin_of(cs, float(np.pi / 2))

            An = sb("An", [128, 5, 2, 8], F32, ps)
            P_("pool", lambda e: e.memset(An[:, 0, 0, :], 1.0))
            P_("pool", lambda e: e.memset(An[:, 0, 1, :], 0.0))
            P_("dve", tt_(An[:, 1, 0, :], mag[:], cs[:], ALU.mult))
            P_("dve", tt_(An[:, 1, 1, :], mag[:], sn[:], ALU.mult))

            def cmul(ore, oim, are, aim, bre, bim):
                P_("dve", tt_(tmp[:], are, bre, ALU.mult))
                P_("dve", tt_(tmp2[:], aim, bim, ALU.mult))
                P_("dve", tt_(ore, tmp[:], tmp2[:], ALU.subtract))
                P_("dve", tt_(tmp[:], are, bim, ALU.mult))
                P_("dve", tt_(tmp2[:], aim, bre, ALU.mult))
                P_("dve", tt_(oim, tmp[:], tmp2[:], ALU.add))

            for n in range(2, 5):
                cmul(An[:, n, 0, :], An[:, n, 1, :], An[:, n - 1, 0, :], An[:, n - 1, 1, :], An[:, 1, 0, :], An[:, 1, 1, :])
            qre = t8("qre"); qim = t8("qim"); q2re = t8("q2re"); q2im = t8("q2im")
            P_("dve", cp_(qre[:], An[:, 4, 0, :]))
            P_("dve", cp_(qim[:], An[:, 4, 1, :]))
            for k in range(8):
                P_("dve", cp_(aRe[:, :, k], qre[:]))
                P_("dve", cp_(aIm[:, :, k], qim[:]))
                P_("dve", ts_(naIm[:, :, k], qim[:], -1.0, None, ALU.mult))
                if k < 7:
                    cmul(q2re[:], q2im[:], qre[:], qim[:], qre[:], qim[:])
                    P_("dve", cp_(qre[:], q2re[:]))
                    P_("dve", cp_(qim[:], q2im[:]))
            den = t8("den"); zre = t8("zre"); zim = t8("zim"); nr = t8("nr")
            P_("dve", tt_(den[:], lr[:], lr[:], ALU.mult))
            P_("dve", tt_(tmp[:], sm("lamim", 0, 8), sm("lamim", 0, 8), ALU.mult))
            P_("dve", tt_(den[:], den[:], tmp[:], ALU.add))
            P_("dve", lambda e: e.reciprocal(out=den[:], in_=den[:]))
            P_("dve", ts_(nr[:], An[:, 1, 0, :], -1.0, None, ALU.add))
            P_("dve", tt_(tmp[:], nr[:], lr[:], ALU.mult))
            P_("dve", tt_(tmp2[:], An[:, 1, 1, :], sm("lamim", 0, 8), ALU.mult))
            P_("dve", tt_(zre[:], tmp[:], tmp2[:], ALU.add))
            P_("dve", tt_(zre[:], zre[:], den[:], ALU.mult))
            P_("dve", tt_(tmp[:], An[:, 1, 1, :], lr[:], ALU.mult))
            P_("dve", tt_(tmp2[:], nr[:], sm("lamim", 0, 8), ALU.mult))
            P_("dve", tt_(zim[:], tmp[:], tmp2[:], ALU.subtract))
            P_("dve", tt_(zim[:], zim[:], den[:], ALU.mult))

            def v3(name, a, b):
                return sb(name, [128, a, b], F32, ps)
            Bbre = v3("Bbre", 8, 16); Bbim = v3("Bbim", 8, 16); t3a = v3("t3a", 8, 16); t3b = v3("t3b", 8, 16)
            Bre3 = sm("Bre", 0, 128).rearrange("p (j h) -> p j h", j=8)
            Bim3 = sm("Bim", 0, 128).rearrange("p (j h) -> p j h", j=8)
            Cre3 = sm("Cre", 0, 128).rearrange("p (j h) -> p j h", j=8)
            Cim3 = sm("Cim", 0, 128).rearrange("p (j h) -> p j h", j=8)

            def bc(ap2):
                return ap2.unsqueeze(2).broadcast_to([128, 8, 16])

            def cmul3(ore, oim, sre, sim, vre, vim, neg_im=False):
                P_("dve", tt_(t3a[:], vre, bc(sre), ALU.mult))
                P_("dve", tt_(t3b[:], vim, bc(sim), ALU.mult))
                P_("dve", tt_(ore, t3a[:], t3b[:], ALU.subtract))
                P_("dve", tt_(t3a[:], vim, bc(sre), ALU.mult))
                P_("dve", tt_(t3b[:], vre, bc(sim), ALU.mult))
                if neg_im:
                    P_("dve", tt_(t3a[:], t3a[:], t3b[:], ALU.add))
                    P_("dve", ts_(oim, t3a[:], -1.0, None, ALU.mult))
                else:
                    P_("dve", tt_(oim, t3a[:], t3b[:], ALU.add))

            cmul3(Bbre[:], Bbim[:], zre[:], zim[:], Bre3, Bim3)

            Tpad = sb("Tpad", [128, 8, 8, 128], F32, ps)
            CApad = sb("CApad", [128, 8, 8, 128], F32, ps)
            Tn = v3("Tn_re", 8, 16); Tni = v3("Tn_im", 8, 16)
            P_("pool", lambda e: e.memset(Tpad[:], 0.0))
            P_("pool", lambda e: e.memset(CApad[:], 0.0))
            for li in range(16):
                emit_cast(li)

            B_scat = []

            def scatter(dst4, sr, src_re, src_im):
                for g2 in range(2):
                    pr = slice(64 * g2, 64 * g2 + 64)
                    for jl in range(4):
                        c0 = jl * 32 + 16 * g2
                        for ri, src in ((0, src_re), (1, src_im)):
                            bsc = Buf("sc")
                            B_scat.append(bsc)
                            tr.op("dve", cp_(dst4[pr, jl::4, sr * 2 + ri, c0:c0 + 16], src[pr, jl::4, :]), reads=[B_p], writes=[bsc])

            for s in range(4):
                n = 3 - s
                cmul3(Tn[:], Tni[:], An[:, n, 0, :], An[:, n, 1, :], Bbre[:], Bbim[:])
                scatter(Tpad, s, Tn, Tni)
            for n in range(5):
                cmul3(Tn[:], Tni[:], An[:, n, 0, :], An[:, n, 1, :], Cre3, Cim3, neg_im=True)
                if n < 4:
                    scatter(CApad, n, Tn, Tni)
                if n >= 1:
                    t = n - 1
                    Wout4 = Wout[:].rearrange("p (j t r) c -> p j (t r) c", j=8, t=4, r=2)
                    for g2 in range(2):
                        pr = slice(64 * g2, 64 * g2 + 64)
                        for jl in range(4):
                            c0 = jl * 32 + 16 * g2
                            for ri, src in ((0, Tn), (1, Tni)):
                                tr.op("dve", cp_(Wout4[pr, jl::4, t * 2 + ri, c0:c0 + 16], src[pr, jl::4, :]),
                                      reads=[B_p], writes=[Buf("scw")])
            for q in range(16):
                Pb, Bb = bank()
                fns = []
                for i in range(4):
                    idx = q * 4 + i
                    fns.append(lambda e, idx=idx, i=i, Pb=Pb: e.transpose(
                        out=Pb[:, i * 128:(i + 1) * 128], in_=Tpad[:, idx // 8, idx % 8, :], identity=ident[:]))
                tr.group("pe", fns, reads=[B_p, B_tab] + B_scat, writes=[Bb])
                tr.op("act", acp_(Wst[:, q * 4:(q + 1) * 4, :], Pb[:].rearrange("p (a c) -> p a c", a=4)),
                      reads=[Bb], writes=[B_tab])
            for kt in range(2):
                Pb, Bb = bank()
                fns = []
                for tau in range(4):
                    lst = [(jp, ri) for jp in range(kt * 4, kt * 4 + 4) for ri in range(2)]
                    for i, (jp, ri) in enumerate(lst):
                        fns.append(mm(Pb[:, tau * 128:(tau + 1) * 128], Tpad[:, jp, 6 + ri, :], CApad[:, jp, tau * 2 + ri, :],
                                      i == 0, i == len(lst) - 1))
                tr.group("pe", fns, reads=[B_p] + B_scat, writes=[Bb])
                tr.op("dve", stt(Kbd[:, kt * 4, :], ident[:], sm("Dcol", kt, 1), Pb[:, 0:128]),
                      reads=[Bb, B_tab, B_small], writes=[B_tab])
                tr.op("act", acp_(Kbd[:, kt * 4 + 1:kt * 4 + 4, :], Pb[:, 128:512].rearrange("p (a c) -> p a c", a=3)),
                      reads=[Bb], writes=[B_tab])
            tr.barrier()
        h = sb("h", [128, 8, SEGT], F32)
        yraw = sb("yraw", [128, 8 * SEGT], BF16)
        y = yraw[:].rearrange("p (k t) -> p k t", k=8)
        ring = [sb("ring%d" % i, [128, 2048], BF16) for i in range(5)]
        B_hb = [Buf("hb%d" % i) for i in range(8)]
        B_h = [B_hb[0:4], B_hb[4:8]]
        B_yjb = [[Buf("y%d_%d" % (j, i)) for i in range(8)] for j in range(8)]
        B_y = [[B_yjb[j][tb] for j in range(8) for tb in range(4 * tt, 4 * tt + 4)] for tt in range(2)]

        def Byj(j, tt):
            return B_yjb[j][4 * tt:4 * tt + 4]
        B_ring = [Buf("ring%d" % i) for i in range(5)]

        def tl(tt):
            return slice(tt * TT, (tt + 1) * TT)

        NRING = 5

        class WS:
            def __init__(self):
                self.nprep = 0
                self.nring = 0
                self.dests = {}
                self.ringslot = {}

            def ready(self, n):
                return True

            def _prepare(self, n):
                li = n % NSLAB
                while ncast[0] < min(NSLAB, n + 17):
                    emit_cast(ncast[0])
                if n in self.dests:
                    dap, db, nel = self.dests[n]
                    k = dap.shape[1]
                    wv = wbf_d[li][:, 0:nel].rearrange("p (k c) -> p k c", k=k)
                    key = "wdst"
                else:
                    r = self.nring % NRING
                    self.nring += 1
                    self.ringslot[n] = r
                    dap, db, nel, k = ring[r][:], B_ring[r], 2048, None
                    wv = wbf_d[li]
                    key = "ring%d" % r
                if False:
                    s_ = n % 2
                    tr.dma("stg%d" % s_, lambda e: e.dma_start(out=stg[s_][:], in_=wsl_d[li]), writes=[B_stg[s_]])
                    src = stg[s_][:, 0:nel]
                    if k is not None:
                        src = src.rearrange("p (k c) -> p k c", k=k)
                    tr.op("act", acp_(dap, src), reads=[B_stg[s_]], writes=[db])
                    if nseg_total > 1:
                        tr.dma("wst%d" % s_, lambda e: e.dma_start(out=wv, in_=dap), reads=[db], writes=[B_wbf[li]])
                else:
                    tr.dma(key, lambda e: e.dma_start(out=dap, in_=wv), reads=[B_wbf[li]], writes=[db])

            def advance(self, i, total):
                while self.nprep <= i + 3 and self.nprep < total and self.ready(self.nprep):
                    self._prepare(self.nprep)
                    self.nprep += 1
                assert self.nprep > i, (self.nprep, i)

        ws = WS()
        ws_total = nseg_total * NSLAB
        ws_pos = [0]

        def next_slab(dest=None):
            i = ws_pos[0]
            ws_pos[0] += 1
            if dest is not None:
                ws.dests[i] = dest
            return i

        def use_slab(i):
            ws.advance(i, ws_total)
            if i in ws.dests:
                return None, None
            r = ws.ringslot[i]
            return ring[r], B_ring[r]

        def norm_stats(tt, out=None, Bout=None):
            R, BR = bank()
            mm_group(R[:, :], [(onesM[:], y[:, j, tl(tt)]) for j in range(8)], reads=[B_y[tt], B_tab], writes=[BR])
            tr.op("act", act_(R[:, :], R[:, :], AF.Ln, bias=epsc[:, 0:1], scale=1.0), reads=[BR, B_tab], writes=[BR])
            if out is not None:
                tr.op("act", act_(out, R[:, :], AF.Exp, scale=-0.5), reads=[BR], writes=[Bout])
                return R, BR
            tr.op("act", act_(R[:, :], R[:, :], AF.Exp, scale=-0.5), reads=[BR], writes=[BR])
            return R, BR

        def norm(gname, l):
            RR = [norm_stats(tt) for tt in range(2)] if NORM_IL else None
            for tt in range(2):
                R, BR = RR[tt] if NORM_IL else norm_stats(tt)
                for j in range(8):
                    tr.op("dve", stt(y[:, j, tl(tt)], h[:, j, tl(tt)], sm(gname, l * 8 + j, 1), R[:, :], ALU.mult, ALU.mult),
                          reads=[B_h[tt], BR, B_small], writes=Byj(j, tt))

        def resid_add(P, BP, m, tt):
            tr.op("dve", tt_(h[:, m, tl(tt)], P[:, :], h[:, m, tl(tt)], ALU.add), reads=[BP, B_h[tt]], writes=[B_h[tt]])
            tr.op("act", act_(y[:, m, tl(tt)], h[:, m, tl(tt)], AF.Square), reads=[B_h[tt]], writes=Byj(m, tt))

        def out_proj(mix, B_mix):
            tr.op("act", act_(dummy[:, 0:1], epsc[:, 0:1], AF.Ln), reads=[B_tab], writes=[B_dummy])
            for sl in range(4):
                i = next_slab()
                rg, Brg = use_slab(i)
                w3 = rg[:].rearrange("p (k c) -> p k c", k=8)
                for ml in range(2):
                    m = sl * 2 + ml
                    for tt in range(2):
                        P, BP = bank()
                        mm_group(P[:, :], [(w3[:, k, ml * 128:(ml + 1) * 128], mix[:, k, tl(tt)]) for k in range(8)],
                                 reads=[Brg, B_mix], writes=[BP])
                        resid_add(P, BP, m, tt)

        def conv_evac(P2, BP2, a, Ba, l, m, sp):
            cw2 = sm("cw_ffn", (l * 44 + m) * 3 + 2, 1)
            cb = sm("cb_ffn", l * 44 + m, 1)
            hout = halo_f[:, sp, l, m, :]
            tr.op("act", act_(a[:, :], P2[:, :], AF.Identity, bias=cb, scale=cw2), reads=BP2 + [B_small], writes=[Ba])
            tr.op("act", acp_(hout[:, :], P2[:, SEGT - 2:SEGT]), reads=BP2, writes=[B_halo_f[sp][l][m]])

        def conv_taps(P2, BP2, a, Ba, l, m):
            cw = lambda k: sm("cw_ffn", (l * 44 + m) * 3 + k, 1)
            tr.op("dve", stt(a[:, 1:SEGT], P2[:, 0:SEGT - 1], cw(1), a[:, 1:SEGT]), reads=BP2 + [Ba, B_small], writes=[Ba])
            tr.op("dve", stt(a[:, 2:SEGT], P2[:, 0:SEGT - 2], cw(0), a[:, 2:SEGT]), reads=BP2 + [Ba], writes=[Ba])

        def conv_corr(a, Ba, l, m, sp):
            cw = lambda k: sm("cw_ffn", (l * 44 + m) * 3 + k, 1)
            hin = halo_f[:, 1 - sp, l, m, :]
            Bhin = B_halo_f[1 - sp][l][m]
            tr.op("act", act_(a[:, 0:1], hin[:, 1:2], AF.Identity, bias=a[:, 0:1], scale=cw(1)), reads=[Bhin, Ba], writes=[Ba])
            tr.op("act", act_(a[:, 0:1], hin[:, 0:1], AF.Identity, bias=a[:, 0:1], scale=cw(0)), reads=[Bhin, Ba], writes=[Ba])
            tr.op("act", act_(a[:, 1:2], hin[:, 1:2], AF.Identity, bias=a[:, 1:2], scale=cw(0)), reads=[Bhin, Ba], writes=[Ba])

        def ffn(l, sp):
            norm("g_ffn", l)
            with ExitStack() as fs:
                actb = sb("actb", [128, NPAIR, SEGT], BF16, fs)
                acc = [sb("acc%d" % i, [128, SEGT], F32, fs) for i in range(4)]
                B_acc = [Buf("acc%d" % i) for i in range(4)]
                B_act = [Buf("act%d" % i) for i in range(NPAIR)]
                prev = None

                def tail_act(i_, ag, Bag, av, Bav):
                    conv_corr(ag, Bag, l, i_, sp)
                    conv_corr(av, Bav, l, NPAIR + i_, sp)
                    tr.op("act", act_(ag[:, :], ag[:, :], AF.Silu), reads=[Bag], writes=[Bag])

                def tail_dve(i_, ag, Bag, av, Bav):
                    tr.op("dve", tt_(actb[:, i_, :], ag[:, :], av[:, :], ALU.mult), reads=[Bag, Bav], writes=[B_act[i_]])

                for i_ in range(NPAIR):
                    i = next_slab()
                    rg, Brg = use_slab(i)
                    w3 = rg[:].rearrange("p (k c) -> p k c", k=8)
                    G2, Gb, BG = bank2()
                    V2, Vb, BV = bank2()
                    if i_ == 0:
                        for tt in range(2):
                            mm_group_y(Gb[tt], lambda k: w3[:, k, 0:128], tt, [Brg], [BG[tt]], perk=True)
                            mm_group_y(Vb[tt], lambda k: w3[:, k, 128:256], tt, [Brg], [BV[tt]], perk=(tt == 1))
                    else:
                        for tt in range(2):
                            mm_group_y(Gb[tt], lambda k: w3[:, k, 0:128], tt, [Brg], [BG[tt]])
                        for tt in range(2):
                            mm_group_y(Vb[tt], lambda k: w3[:, k, 128:256], tt, [Brg], [BV[tt]])
                    par = i_ % 2
                    ag, Bag = acc[par * 2], B_acc[par * 2]
                    av, Bav = acc[par * 2 + 1], B_acc[par * 2 + 1]
                    conv_evac(G2, BG, ag, Bag, l, i_, sp)
                    conv_taps(G2, BG, ag, Bag, l, i_)
                    if prev is not None:
                        tail_act(*prev)
                    conv_evac(V2, BV, av, Bav, l, NPAIR + i_, sp)
                    if prev is not None:
                        tail_dve(*prev)
                    conv_taps(V2, BV, av, Bav, l, NPAIR + i_)
                    prev = (i_, ag, Bag, av, Bav)
                tail_act(*prev)
                tail_dve(*prev)
                for m in range(8):
                    i0 = next_slab()
                    i1 = next_slab()
                    r0, Br0 = use_slab(i0)
                    r1, Br1 = use_slab(i1)
                    w0 = r0[:, 0:1408].rearrange("p (k c) -> p k c", k=11)
                    w1 = r1[:, 0:1408].rearrange("p (k c) -> p k c", k=11)
                    if m == 0:
                        tr.op("act", act_(dummy[:, 0:1], epsc[:, 0:1], AF.Ln), reads=[B_tab], writes=[B_dummy])
                    PB = [bank() for tt in range(2)]
                    NA = 16
                    for part in range(2):
                        ks = range(0, NA) if part == 0 else range(NA, 22)
                        for tt in range(2):
                            P, BP = PB[tt]
                            fns = []
                            for k in ks:
                                wk = w0[:, k, :] if k < 11 else w1[:, k - 11, :]
                                fns.append(mm(P[:, :], wk, actb[:, k, tl(tt)], k == 0, k == 21))
                            tr.group("pe", fns, reads=[Br0, Br1] + [B_act[k] for k in ks], writes=[BP])
                    for tt in range(2):
                        P, BP = PB[tt]
                        resid_add(P, BP, m, tt)
                tr.barrier()

        def mixer0():
            norm("g_mix", 0)
            with ExitStack() as ms:
                mix = sb("mix", [128, 8, SEGT], BF16, ms)
                ubf = sb("ubf", [128, 2, SEGT], BF16, ms)
                R12 = sb("R12", [128, 3400], F32, ms)
                Sprev = sb("Sprev", [128, 8, 2, NB], BF16, ms)
                vg = sb("vg", [128, 8, 768], BF16, ms)
                gvt = [sb("gvt%d" % i, [128, 256], F32, ms) for i in range(2)]
                junk = sb("junk", [128, 256], BF16, ms)
                ssq3 = sb("ssq3", [128, 8, 3], F32, ms)
                rstdv = sb("rstdv", [128, 8], F32, ms)
                gu = [sb("gu%d" % i, [128, 512], F32, ms) for i in range(12)]
                SP_ = 32
                XG = R12[:, 0:1024].rearrange("p (q r b) -> p q r b", q=2, r=2)
                XH = R12[:, 1024:1536].rearrange("p (q r b) -> p q r b", q=2, r=2)
                XS = [R12[:, 1536 + i * 384:1536 + (i + 1) * 384].rearrange("p (q r b) -> p q r b", q=2, r=2) for i in range(2)]
                XT = R12[:, 2304:2824].rearrange("p (q r b) -> p q r b", q=2, r=2)
                XM = R12[:, 2824:3336].rearrange("p (q r b) -> p q r b", q=2, r=2)
                ysf = R12[:, 0:2048].rearrange("p (k t) -> p k t", k=2)
                ybf = R12[:, 2048:3072].bitcast(BF16).rearrange("p (k t) -> p k t", k=2)
                B_mix = Buf("mix"); B_ubf = Buf("ubf"); B_Sprev = Buf("Sprev"); B_ysf = Buf("ysf"); B_ybf = Buf("ybf")
                def qr(nm):
                    return [[Buf("%s_%d_%d" % (nm, q, r)) for r in range(2)] for q in range(2)]
                B_G = qr("G"); B_H = qr("H"); B_S = [qr("S0"), qr("S1")]; B_T = qr("T"); B_M = qr("M")
                allX = [b for grp in (B_G, B_H, B_S[0], B_S[1], B_T, B_M) for qq in grp for b in qq]
                B_vg = [Buf("vg%d" % n) for n in range(8)]
                B_gvt = [Buf("gvt0"), Buf("gvt1")]; B_junk = Buf("junk"); B_ssq = Buf("ssq"); B_gu = [Buf("gu%d" % i) for i in range(12)]

                def s5_gen():
                    si_ = next_slab()
                    rg, Brg = use_slab(si_)
                    w3 = rg[:].rearrange("p (k c) -> p k c", k=8)
                    for m in range(2):
                        for tt in range(2):
                            P, BP = bank()
                            mm_group_y(P[:, :], lambda k: w3[:, k, m * 128:(m + 1) * 128], tt, [Brg], [BP], perk=(m == 0))
                            tr.op("act", acp_(ubf[:, m, tl(tt)], P[:, :]), reads=[BP], writes=[B_ubf])
                    for i in range(2):
                        tr.op("dve", lambda e, i=i: e.memset(XS[i][:, :, :, 0:SP_], 0.0), writes=[B_S[i][q][r] for q in range(2) for r in range(2)])
                    tr.op("dve", lambda e: e.memset(XT[:, :, :, 0:1], 0.0), writes=[B_T[q][r] for q in range(2) for r in range(2)])
                    yield

                    def cma(jps, k, n, out, B_out, X, B_Xs, Ad, B_Ads, tmp=None, B_tmp=None):
                        t_ = tmp if tmp is not None else out
                        Bt = B_tmp if B_tmp is not None else B_out
                        for q, jp in enumerate(jps):
                            for r in range(2):
                                tr.op("dve", stt(t_(q, r), X(q, r), aRe[:, jp, k:k + 1], Ad(q, r)),
                                      reads=B_Xs(q, r) + B_Ads(q, r) + [B_tab], writes=Bt(q, r))
                        for q, jp in enumerate(jps):
                            tr.op("dve", stt(out(q, 0), X(q, 1), naIm[:, jp, k:k + 1], t_(q, 0)),
                                  reads=B_Xs(q, 1) + Bt(q, 0), writes=B_out(q, 0))
                            tr.op("dve", stt(out(q, 1), X(q, 0), aIm[:, jp, k:k + 1], t_(q, 1)),
                                  reads=B_Xs(q, 0) + Bt(q, 1), writes=B_out(q, 1))

                    for rnd in range(4):
                        jps = [rnd * 2, rnd * 2 + 1]
                        for q, jp in enumerate(jps):
                            kt = jp // 4
                            P, BP = bank()
                            fns = []
                            for ri in range(2):
                                for s_ in range(4):
                                    fns.append(mm(P[:, ri * NB:(ri + 1) * NB], Wst[:, jp * 8 + s_ * 2 + ri, :],
                                                  ubf[:, kt, s_::4], s_ == 0, s_ == 3))
                            tr.group("pe", fns, reads=[B_ubf, B_tab], writes=[BP])
                            tr.op("act", acp_(XG[:, q, :, :], P[:, :].rearrange("p (r b) -> p r b", r=2)),
                                  reads=[BP], writes=[B_G[q][0], B_G[q][1]])
                            x0r = XG[:, q, 0, 0:1]
                            x0i = XG[:, q, 1, 0:1]
                            cr = carry[:, jp, 0:1]
                            ci = carry[:, jp, 1:2]
                            tr.op("dve", stt(x0r, cr, aRe[:, jp, 0:1], x0r), reads=[B_carry, B_tab], writes=[B_G[q][0]])
                            tr.op("dve", stt(x0i, ci, aRe[:, jp, 0:1], x0i), reads=[B_carry, B_tab], writes=[B_G[q][1]])
                            tr.op("dve", stt(x0r, ci, naIm[:, jp, 0:1], x0r), reads=[B_carry], writes=[B_G[q][0]])
                            tr.op("dve", stt(x0i, cr, aIm[:, jp, 0:1], x0i), reads=[B_carry], writes=[B_G[q][1]])
                        yield
                        cma(jps, 0, 128, lambda q, r: XH[:, q, r, :], lambda q, r: [B_H[q][r]],
                            lambda q, r: XG[:, q, r, 0::2], lambda q, r: [B_G[q][r]],
                            lambda q, r: XG[:, q, r, 1::2], lambda q, r: [B_G[q][r]])
                        yield
                        cma(jps, 1, 64, lambda q, r: XS[0][:, q, r, SP_:SP_ + 64], lambda q, r: [B_S[0][q][r]],
                            lambda q, r: XH[:, q, r, 0::2], lambda q, r: [B_H[q][r]],
                            lambda q, r: XH[:, q, r, 1::2], lambda q, r: [B_H[q][r]])
                        yield
                        src, dst = 0, 1
                        for s_ in range(6):
                            d = 1 << s_
                            cma(jps, s_ + 2, 64,
                                lambda q, r, dst=dst: XS[dst][:, q, r, SP_:SP_ + 64], lambda q, r, dst=dst: [B_S[dst][q][r]],
                                lambda q, r, src=src, d=d: XS[src][:, q, r, SP_ - d:SP_ - d + 64], lambda q, r, src=src: [B_S[src][q][r]],
                                lambda q, r, src=src: XS[src][:, q, r, SP_:SP_ + 64], lambda q, r, src=src: [B_S[src][q][r]])
                            src, dst = dst, src
                            yield
                        cma(jps, 1, 64, lambda q, r: XT[:, q, r, 1:129:2], lambda q, r: [B_T[q][r]],
                            lambda q, r, src=src: XS[src][:, q, r, SP_ - 1:SP_ + 63], lambda q, r, src=src: [B_S[src][q][r]],
                            lambda q, r: XH[:, q, r, 0::2], lambda q, r: [B_H[q][r]])
                        for q, jp in enumerate(jps):
                            tr.op("act", acp_(XT[:, q, :, 2:130:2], XS[src][:, q, :, SP_:SP_ + 64]),
                                  reads=[B_S[src][q][0], B_S[src][q][1]], writes=[B_T[q][0], B_T[q][1]])
                        yield
                        cma(jps, 0, 128, lambda q, r: Sprev[:, jps[q], r, 1::2], lambda q, r: [B_Sprev],
                            lambda q, r: XT[:, q, r, 0:128], lambda q, r: [B_T[q][r]],
                            lambda q, r: XG[:, q, r, 0::2], lambda q, r: [B_G[q][r]],
                            tmp=lambda q, r: XM[:, q, r, :], B_tmp=lambda q, r: [B_M[q][r]])
                        for q, jp in enumerate(jps):
                            rb = [B_T[q][0], B_T[q][1]]
                            tr.op("act", acp_(Sprev[:, jp, :, 2::2], XT[:, q, :, 1:128]), reads=rb, writes=[B_Sprev])
                            tr.op("act", acp_(Sprev[:, jp, :, 0:1], carry[:, jp, :].unsqueeze(2)), reads=[B_carry], writes=[B_Sprev])
                            tr.op("act", acp_(carry[:, jp, :].unsqueeze(2), XS[src][:, q, :, SP_ + 63:SP_ + 64]),
                                  reads=[B_S[src][q][0], B_S[src][q][1]], writes=[B_carry])
                        yield
                    for kt in range(2):
                        for hb in range(2):
                            P, BP = bank()
                            fns = []
                            for t in range(4):
                                lst = []
                                for s_ in range(t + 1):
                                    lst.append((Kbd[:, kt * 4 + (t - s_), :], ubf[:, kt, hb * 512 + s_:hb * 512 + 512:4]))
                                for jl in range(4):
                                    jp = kt * 4 + jl
                                    for ri in range(2):
                                        lst.append((Wout[:, jp * 8 + t * 2 + ri, :], Sprev[:, jp, ri, hb * 128:(hb + 1) * 128]))
                                for i, (l_, r_) in enumerate(lst):
                                    fns.append(mm(P[:, t::4], l_, r_, i == 0, i == len(lst) - 1))
                            tr.group("pe", fns, reads=[B_ubf, B_Sprev, B_tab], writes=[BP])
                            tr.op("act", act_(ysf[:, kt, tl(hb)], P[:, :], AF.Gelu_apprx_tanh), reads=[BP], writes=[B_ysf] + allX)
                            tr.op("act", acp_(ybf[:, kt, tl(hb)], ysf[:, kt, tl(hb)]), reads=[B_ysf], writes=[B_ybf] + allX)
                            yield
                    Ps = []
                    for m in range(2):
                        for tt in range(2):
                            P, BP = bank()
                            mm_group(P[:, :], [(wglu[:, k, m * 128:(m + 1) * 128], ybf[:, k, tl(tt)]) for k in range(2)],
                                     reads=[B_ybf, B_tab], writes=[BP])
                            Ps.append((P, BP, m, tt))
                    for P, BP, m, tt in Ps:
                        tr.op("act", act_(P[:, :], P[:, :], AF.Sigmoid, bias=sm("bglu", m, 1), scale=1.0), reads=[BP, B_small], writes=[BP])
                    for P, BP, m, tt in Ps:
                        tr.op("dve", tt_(mix[:, m, tl(tt)], ysf[:, m, tl(tt)], P[:, :], ALU.mult), reads=[B_ysf, BP], writes=[B_mix])
                    yield

                def gm_gen():
                    for sl in range(3):
                        i = next_slab()
                        rg, Brg = use_slab(i)
                        w3 = rg[:].rearrange("p (k c) -> p k c", k=8)
                        for n in range(8):
                            tok = slice(n * 128, (n + 1) * 128)
                            tt = n // 4
                            P, BP = bank()
                            mm_group(P[:, 0:256], [(y[:, k, tok], w3[:, k, :]) for k in range(8)], reads=[B_y[tt], Brg], writes=[BP])
                            u = (sl * 8 + n) % 2
                            g_ = gvt[u]
                            tr.op("act", act_(g_[:, :], P[:, 0:256], AF.Gelu_apprx_tanh), reads=[BP], writes=[B_gvt[u]])
                            tr.op("act", lambda e, g_=g_, n=n, sl=sl: e.activation(out=junk[:, :], in_=g_[:, :], func=AF.Square,
                                                                                 accum_out=ssq3[:, n, sl:sl + 1]),
                                  reads=[B_gvt[u]], writes=[B_junk, B_ssq])
                            tr.op("dve", tt_(vg[:, n, sl * 256:(sl + 1) * 256], g_[:, :], gvg[:, sl * 256:(sl + 1) * 256], ALU.mult),
                                  reads=[B_gvt[u], B_tab], writes=[B_vg[n]])
                            yield
                    tr.op("dve", tt_(rstdv[:, :], ssq3[:, :, 0], ssq3[:, :, 1], ALU.add), reads=[B_ssq], writes=[B_ssq])
                    tr.op("dve", tt_(rstdv[:, :], rstdv[:, :], ssq3[:, :, 2], ALU.add), reads=[B_ssq], writes=[B_ssq])
                    tr.op("act", act_(rstdv[:, :], rstdv[:, :], AF.Ln, bias=epsc[:, 0:1], scale=1.0 / 768.0), reads=[B_ssq, B_tab], writes=[B_ssq])
                    tr.op("act", act_(rstdv[:, :], rstdv[:, :], AF.Exp, scale=-0.5), reads=[B_ssq], writes=[B_ssq])
                    for n in range(8):
                        tr.op("act", act_(vg[:, n, :], vg[:, n, :], AF.Copy, scale=rstdv[:, n:n + 1]), reads=[B_ssq, B_vg[n]], writes=[B_vg[n]])
                    yield
                    units = []
                    for sl in range(3):
                        i = next_slab()
                        rg, Brg = use_slab(i)
                        w3 = rg[:].rearrange("p (k c) -> p k c", k=8)
                        for hl in range(2):
                            hd = sl * 2 + hl
                            for tt in range(2):
                                u = len(units)
                                PU, BPU = bank()
                                mm_group(PU[:, :], [(w3[:, k, hl * 128:(hl + 1) * 128], y[:, k, tl(tt)]) for k in range(8)],
                                         reads=[Brg, B_y[tt]], writes=[BPU])
                                tr.op("act", act_(gu[u][:, :], PU[:, :], AF.Gelu_apprx_tanh), reads=[BPU], writes=[B_gu[u]])
                                units.append((hd, tt, u))
                                yield
                    for hd, tt, u in units:
                        PG, BPG = bank()
                        fns = []
                        for c4 in range(4):
                            n = tt * 4 + c4
                            o = PG[:, c4 * 128:(c4 + 1) * 128]
                            fns.append(mm(o, vg[:, n, hd * 128:(hd + 1) * 128], WsT[:, hd, :], True, False))
                            fns.append(mm(o, ones2[:, :], bs2[:, hd * 128:(hd + 1) * 128], False, True))
                        tr.group("pe", fns, reads=[B_vg[tt * 4 + c] for c in range(4)] + [B_tab, B_small], writes=[BPG])
                        tr.op("dve", tt_(mix[:, 2 + hd, tl(tt)], gu[u][:, :], PG[:, :], ALU.mult), reads=[B_gu[u], BPG], writes=[B_mix])
                        yield

                gens = [s5_gen(), gm_gen()]
                while gens:
                    for g in list(gens):
                        try:
                            next(g)
                        except StopIteration:
                            gens.remove(g)
                tr.barrier()
                out_proj(mix, B_mix)
                tr.barrier()

        def mixer1():
            norm("g_mix", 1)
            with ExitStack() as ms:
                mix = sb("mix1", [128, 8, SEGT], BF16, ms)
                B_mix = Buf("mix1")
                hxs = [sb("hxs%d" % i, [128, 512], F32, ms) for i in range(2)]
                qb = [sb("qb%d" % i, [128, 514], F32, ms) for i in range(2)]
                cc = [sb("cc%d" % i, [128, 512], F32, ms) for i in range(2)]
                B_hxs = [Buf("hxs0"), Buf("hxs1")]; B_qb = [Buf("qb0"), Buf("qb1")]; B_cc = [Buf("cc0"), Buf("cc1")]
                slabB = None
                pending1 = []

                def tail1(j, tt, PBg, BPBg):
                    q_, c_ = qb[tt], cc[tt]
                    cw = lambda k: sm("cw_od", j * 3 + k, 1)
                    cb = sm("cb_od", j, 1)
                    tr.op("act", act_(c_[:, :], q_[:, 2:514], AF.Identity, bias=cb, scale=cw(2)), reads=[B_qb[tt], B_small], writes=[B_cc[tt]])
                    tr.op("dve", stt(c_[:, :], q_[:, 1:513], cw(1), c_[:, :]), reads=[B_qb[tt], B_cc[tt]], writes=[B_cc[tt]])
                    tr.op("dve", stt(c_[:, :], q_[:, 0:512], cw(0), c_[:, :]), reads=[B_qb[tt], B_cc[tt]], writes=[B_cc[tt]])
                    tr.op("dve", tt_(mix[:, j, tl(tt)], c_[:, :], PBg[:, :], ALU.mult), reads=[B_cc[tt], BPBg], writes=[B_mix])

                for j in range(8):
                    iA = next_slab()
                    if j % 2 == 0:
                        iB = next_slab()
                    rA, BrA = use_slab(iA)
                    if j % 2 == 0:
                        rB, BrB = use_slab(iB)
                        slabB = (rB, BrB)
                    rB, BrB = slabB
                    wA = rA[:].rearrange("p (k c) -> p k c", k=8)
                    wB = rB[:].rearrange("p (k c) -> p k c", k=8)
                    bo = (j % 2) * 128
                    cw = lambda k: sm("cw_od", j * 3 + k, 1)
                    cb = sm("cb_od", j, 1)
                    for tt in range(2):
                        PC, BPC = bank()
                        mm_group_y(PC[:, :], lambda k: wA[:, k, 0:128], tt, [BrA], [BPC], perk=(j == 0))
                        PH, BPH = bank()
                        mm_group_y(PH[:, :], lambda k: wA[:, k, 128:256], tt, [BrA], [BPH], perk=(j == 0))
                        PBg, BPBg = bank()
                        mm_group_y(PBg[:, :], lambda k: wB[:, k, bo:bo + 128], tt, [BrB], [BPBg], perk=(j == 0))
                        hx_, q_, c_ = hxs[tt], qb[tt], cc[tt]
                        tr.op("act", acp_(hx_[:, :], PH[:, :]), reads=[BPH], writes=[B_hxs[tt]])
                        tr.op("dve", cp_(q_[:, 0:2], halo_q[:, 1 - tt, j, :]), reads=[B_halo_q[1 - tt][j]], writes=[B_qb[tt]])
                        tr.op("dve", tt_(q_[:, 2:514], PC[:, :], hx_[:, :], ALU.mult), reads=[BPC, B_hxs[tt]], writes=[B_qb[tt]])
                        tr.op("dve", cp_(halo_q[:, tt, j, :], q_[:, 512:514]), reads=[B_qb[tt]], writes=[B_halo_q[tt][j]])
                        pending1.append((j, tt, PBg, BPBg))
                        if len(pending1) > (1 if PIPE_L1 else 0):
                            tail1(*pending1.pop(0))
                while pending1:
                    tail1(*pending1.pop(0))
                out_proj(mix, B_mix)
                tr.barrier()

        def seg_pos(si):
            return si // NSEGS, (si % NSEGS) * SEGT

        def load_dma(si, tb, xb, Bx, key):
            sq, t0 = seg_pos(si)
            tr.dma(key, lambda e: e.dma_start(out=xb[:], in_=x_d[sq, t0 + tb * 128:t0 + (tb + 1) * 128, :]), writes=[Bx])

        def load_compute(tb, xb, Bx):
            tt = tb // 4
            for jh in range(2):
                P, BP = bank()
                fns = [(lambda e, jl=jl, P=P, xb=xb, jh=jh: e.transpose(out=P[:, jl * 128:(jl + 1) * 128],
                                                                        in_=xb[:, (jh * 4 + jl) * 128:(jh * 4 + jl + 1) * 128],
                                                                        identity=ident[:])) for jl in range(4)]
                tr.group("pe", fns, reads=[Bx, B_tab], writes=[BP])
                tr.op("act", acp_(h[:, jh * 4:(jh + 1) * 4, tb * 128:(tb + 1) * 128], P[:, :].rearrange("p (a c) -> p a c", a=4)),
                      reads=[BP], writes=[B_hb[tb]])
                tr.op("act", act_(y[:, jh * 4:(jh + 1) * 4, tb * 128:(tb + 1) * 128], h[:, jh * 4:(jh + 1) * 4, tb * 128:(tb + 1) * 128], AF.Square),
                      reads=[B_hb[tb]], writes=[B_yjb[j][tb] for j in range(jh * 4, jh * 4 + 4)])

        for si in range(nseg_total):
            sq, t0 = seg_pos(si)
            if si % NSEGS == 0:
                tr.op("dve", lambda e: e.memset(carry[:], 0.0), writes=[B_carry])
                tr.op("dve", lambda e: e.memset(halo_f[:], 0.0), writes=[b for pp in B_halo_f for l in pp for b in l])
                tr.op("dve", lambda e: e.memset(halo_q[:], 0.0), writes=[b for pp in B_halo_q for b in pp])
            if si == 0:
                with ExitStack() as ls:
                    xi = [sb("xi0_%d" % i, [128, 1024], F32, ls) for i in range(4)]
                    B_xi = [Buf("xi0_%d" % i) for i in range(4)]
                    for tb in range(4):
                        load_dma(0, tb, xi[tb], B_xi[tb], "xi%d" % tb)
                    for tb in range(8):
                        load_compute(tb, xi[tb % 4], B_xi[tb % 4])
                        if tb + 4 < 8:
                            load_dma(0, tb + 4, xi[tb % 4], B_xi[tb % 4], "xi%d" % (tb % 4))
                    tr.barrier()
            if stop_after != "load":
                mixer0()
            if stop_after not in ("load", "mix0"):
                ffn(0, si % 2)
            if stop_after not in ("load", "mix0", "ffn0"):
                mixer1()
            if stop_after is None:
                ffn(1, si % 2)
            nxt = si + 1 if si + 1 < nseg_total else None
            with ExitStack() as bs_:
                xi = [sb("xi_%d" % i, [128, 1024], F32, bs_) for i in range(4)]
                ot = [sb("ot_%d" % i, [128, 1024], F32, bs_) for i in range(2)]
                ofb = [sb("ofb_%d" % i, [128, 8, 128], F32, bs_) for i in range(2)]
                rsb = sb("rsb", [128, SEGT], F32, bs_)
                B_xi = [Buf("xi%d" % i) for i in range(4)]
                B_ot = [Buf("ot0"), Buf("ot1")]
                B_ofb = [Buf("ofb0"), Buf("ofb1")]
                B_rsb = [Buf("rsb0"), Buf("rsb1")]
                if nxt is not None:
                    for tb in range(4):
                        load_dma(nxt, tb, xi[tb], B_xi[tb], "xi%d" % tb)
                if stop_after is None:
                    for tt in range(2):
                        norm_stats(tt, out=rsb[:, tl(tt)], Bout=B_rsb[tt])
                for tb in range(8):
                    tt = tb // 4
                    ob = ofb[tb % 2]
                    Bo = B_ofb[tb % 2]
                    tok = slice(tb * 128, (tb + 1) * 128)
                    if stop_after is None:
                        for j in range(8):
                            tr.op("dve", stt(ob[:, j, :], h[:, j, tok], sm("g_fin", j, 1), rsb[:, tok], ALU.mult, ALU.mult),
                                  reads=[B_hb[tb], B_rsb[tt], B_small], writes=[Bo])
                    else:
                        tr.op("dve", cp_(ob[:, :, :], h[:, :, tok]), reads=[B_hb[tb]], writes=[Bo])
                    xb = ot[tb % 2]
                    Bx = B_ot[tb % 2]
                    for jh in range(2):
                        P, BP = bank()
                        fns = [(lambda e, jl=jl, P=P, ob=ob, jh=jh: e.transpose(out=P[:, jl * 128:(jl + 1) * 128],
                                                                                in_=ob[:, jh * 4 + jl, :], identity=ident[:])) for jl in range(4)]
                        tr.group("pe", fns, reads=[Bo, B_tab], writes=[BP])
                        tr.op("act", acp_(xb[:, jh * 512:(jh + 1) * 512], P[:, :]), reads=[BP], writes=[Bx])
                    tr.dma("ot%d" % (tb % 2), lambda e, xb=xb, tb=tb: e.dma_start(out=out_d[sq, t0 + tb * 128:t0 + (tb + 1) * 128, :], in_=xb[:]),
                           reads=[Bx])
                    if nxt is not None:
                        load_compute(tb, xi[tb % 4], B_xi[tb % 4])
                        if tb + 4 < 8:
                            load_dma(nxt, tb + 4, xi[tb % 4], B_xi[tb % 4], "xi%d" % (tb % 4))
                tr.barrier()
        tr.barrier(final=True)
    return nc


def _kslab(W, cols):
    sub = W[:, cols]
    return sub.reshape(8, 128, 256).transpose(1, 0, 2).reshape(128, 2048)


def _dslab(Wd, m, kh):
    sub = Wd[kh * 1408:(kh + 1) * 1408, m * 128:(m + 1) * 128].reshape(11, 128, 128).transpose(1, 0, 2).reshape(128, 1408)
    out = np.zeros((128, 2048), np.float32)
    out[:, :1408] = sub
    return out


def build_slabs(ev_w_in, ev_w_out, od_w_in, od_w_out, ffn_w_up, ffn_w_down):
    slabs = []
    ar = np.arange
    Win0 = ev_w_in[0]
    slabs.append(_kslab(Win0, ar(0, 256)))
    for c in range(3):
        slabs.append(_kslab(Win0, 1024 + c * 256 + ar(256)))
    for c in range(3):
        slabs.append(_kslab(Win0, 256 + c * 256 + ar(256)))
    for c in range(4):
        slabs.append(_kslab(ev_w_out[0], c * 256 + ar(256)))

    def ffn_slabs(l):
        for i in range(NPAIR):
            cols = np.concatenate([i * 128 + ar(128), DFF + i * 128 + ar(128)])
            slabs.append(_kslab(ffn_w_up[l], cols))
        for m in range(8):
            for kh in range(2):
                slabs.append(_dslab(ffn_w_down[l], m, kh))
    ffn_slabs(0)
    Win1 = od_w_in[0]
    for j in range(8):
        slabs.append(_kslab(Win1, np.concatenate([1024 + j * 128 + ar(128), 2048 + j * 128 + ar(128)])))
        if j % 2 == 0:
            slabs.append(_kslab(Win1, j * 128 + ar(256)))
    for c in range(4):
        slabs.append(_kslab(od_w_out[0], c * 256 + ar(256)))
    ffn_slabs(1)
    assert len(slabs) == NSLAB
    return np.ascontiguousarray(np.stack(slabs, 0).astype(np.float32))


def build_small(inp):
    sp = np.zeros((128, NSMALL), np.float32)

    def put(name, arr):
        arr = np.asarray(arr, np.float32)
        sp[:, SOFF[name]:SOFF[name] + arr.shape[1]] = arr

    put("g_mix", inp["mix_norm_g"].reshape(2, 8, 128).transpose(2, 0, 1).reshape(128, 16))
    put("g_ffn", inp["ffn_norm_g"].reshape(2, 8, 128).transpose(2, 0, 1).reshape(128, 16))
    put("g_fin", inp["final_norm_g"].reshape(8, 128).T)

    def gp(a):
        return a.reshape(8, 2, 64).transpose(1, 2, 0).reshape(128, 8)
    put("lamre", gp(inp["s5_lam_re"][0]))
    put("lamim", gp(inp["s5_lam_im"][0]))
    put("logdt", gp(np.repeat(inp["s5_log_dt"][0][:, None], 64, axis=1)))

    def gph(a):
        return a.reshape(8, 2, 64, 16).transpose(1, 2, 0, 3).reshape(128, 128)
    put("Bre", gph(inp["s5_b_re"][0]))
    put("Bim", gph(inp["s5_b_im"][0]))
    put("Cre", gph(inp["s5_c_re"][0].transpose(0, 2, 1)))
    put("Cim", gph(inp["s5_c_im"][0].transpose(0, 2, 1)))
    put("Dcol", inp["s5_d"][0].reshape(2, 128).T)
    put("bglu", inp["s5_b_glu"][0].reshape(2, 128).T)
    put("cw_ffn", inp["ffn_conv_w"].reshape(2, 3, 44, 128).transpose(3, 0, 2, 1).reshape(128, 264))
    put("cb_ffn", inp["ffn_conv_b"].reshape(2, 44, 128).transpose(2, 0, 1).reshape(128, 88))
    put("cw_od", inp["od_conv_w"][0].reshape(3, 8, 128).transpose(2, 1, 0).reshape(128, 24))
    put("cb_od", inp["od_conv_b"][0].reshape(8, 128).T)
    med = np.zeros((128, NMED), np.float32)
    med[:, 0:768] = np.repeat(inp["gm_v_g"][0][None, :], 128, axis=0)
    med[:, 768:1536] = inp["gm_w_s"][0].transpose(2, 0, 1).reshape(128, 768)
    med[:, 1536:2048] = inp["s5_w_glu"][0].reshape(2, 128, 256).transpose(1, 0, 2).reshape(128, 512)
    med[:, 2048:2176] = np.eye(128, dtype=np.float32)
    med[:, 2176:2304] = np.triu(np.ones((128, 128), np.float32))
    bsrow = np.ascontiguousarray(inp["gm_b_s"][0].reshape(1, 768).astype(np.float32))
    return sp, med, bsrow


_NC_CACHE = {}


def kernel(**inputs):
    inp = {k: np.asarray(v) for k, v in inputs.items()}
    x = inp["x"].astype(np.float32, copy=False)
    wsl = build_slabs(inp["ev_w_in"], inp["ev_w_out"], inp["od_w_in"], inp["od_w_out"], inp["ffn_w_up"], inp["ffn_w_down"])
    sp, med, bsrow = build_small(inp)
    if "full" not in _NC_CACHE:
        _NC_CACHE["full"] = build_program()
    nc = _NC_CACHE["full"]
    in_maps = []
    for c in range(NCORES):
        in_maps.append({"x": np.ascontiguousarray(x[c * NSEQ:(c + 1) * NSEQ]), "wsl": wsl, "smallp": sp, "medp": med, "bsrow": bsrow})
    res = run_bass_kernel_spmd(nc, in_maps, core_ids=list(range(NCORES)))
    out = np.concatenate([np.asarray(r["out"]) for r in res.results], axis=0)
    return out.astype(np.float32, copy=False)
```

```python
import os
import numpy as np
from contextlib import ExitStack
import concourse.bass as bass
import concourse.mybir as mybir
from concourse.bass_utils import run_bass_kernel_spmd

F32 = mybir.dt.float32
BF16 = mybir.dt.bfloat16
I32 = mybir.dt.int32
AF = mybir.ActivationFunctionType
ALU = mybir.AluOpType

NCORES = 8
D = 1024
SEQ = 4096
NSEQ = 2
SEGT = 1024
NSEGS = SEQ // SEGT
TT = 512
DFF = 2816
NPAIR = DFF // 128
EPS = 1e-6
NSLAB = 103
BS = 4
NB = SEGT // BS
PAD = 128
PIPE_FFN = os.environ.get('K_PIPE_FFN', '1') == '1'
PIPE_L1 = os.environ.get('K_PIPE_L1', '1') == '1'
NORM_IL = os.environ.get('K_NORM_IL', '1') == '1'


class Buf:
    __slots__ = ("name", "w", "r")

    def __init__(self, name):
        self.name = name
        self.w = None
        self.r = {}


class Eng:
    def __init__(self, name, eng, sem, selfwait):
        self.name = name
        self.eng = eng
        self.sem = sem
        self.count = 0
        self.seen = {}
        self.selfwait = selfwait


class Tracker:
    def __init__(self, nc, stack):
        self.nc = nc
        self.stack = stack
        self.engs = {}
        for name, eng, sw in (("pe", nc.tensor, False), ("act", nc.scalar, True),
                              ("dve", nc.vector, True), ("pool", nc.gpsimd, True),
                              ("sp", nc.sync, True)):
            sem = stack.enter_context(nc.semaphore("sem_" + name))
            self.engs[name] = Eng(name, eng, sem, sw)
        self.dma_sems = {}

    def dma_sem(self, key):
        if key not in self.dma_sems:
            sem = self.stack.enter_context(self.nc.semaphore("dq_" + str(key)))
            self.dma_sems[key] = [sem, 0]
        return self.dma_sems[key]

    def _wait(self, E, deps):
        best = {}
        for d in deps:
            if d is None:
                continue
            k, v = d
            if best.get(k, 0) < v:
                best[k] = v
        for k, v in best.items():
            if (not E.selfwait) and k is E.sem:
                continue
            if E.seen.get(k, 0) < v:
                E.eng.wait_ge(k, v)
                E.seen[k] = v

    @staticmethod
    def _flat(bufs):
        out = []
        for b in bufs:
            if isinstance(b, (list, tuple)):
                out.extend(Tracker._flat(b))
            else:
                out.append(b)
        return out

    @staticmethod
    def _deps(reads, writes):
        reads = Tracker._flat(reads)
        writes = Tracker._flat(writes)
        deps = []
        for b in reads:
            deps.append(b.w)
        for b in writes:
            deps.append(b.w)
            for k, v in b.r.items():
                deps.append((k, v))
        return deps

    @staticmethod
    def _commit(tok, reads, writes):
        reads = Tracker._flat(reads)
        writes = Tracker._flat(writes)
        k, v = tok
        for b in reads:
            if b.r.get(k, 0) < v:
                b.r[k] = v
        for b in writes:
            b.w = tok
            b.r = {}

    def op(self, ename, fn, reads=(), writes=()):
        E = self.engs[ename]
        self._wait(E, self._deps(reads, writes))
        inst = fn(E.eng)
        E.count += 1
        inst.then_inc(E.sem, 1)
        self._commit((E.sem, E.count), reads, writes)
        return inst

    def group(self, ename, fns, reads=(), writes=()):
        E = self.engs[ename]
        self._wait(E, self._deps(reads, writes))
        inst = None
        for fn in fns:
            inst = fn(E.eng)
        E.count += 1
        inst.then_inc(E.sem, 1)
        self._commit((E.sem, E.count), reads, writes)
        return inst

    def dma(self, key, fn, reads=(), writes=(), ename="sp"):
        E = self.engs[ename]
        ent = self.dma_sem(key)
        self._wait(E, self._deps(reads, writes) + [(ent[0], ent[1])])
        inst = fn(E.eng)
        ent[1] += 16
        inst.then_inc(ent[0], 16)
        self._commit((ent[0], ent[1]), reads, writes)
        return inst

    def barrier(self, final=False):
        toks = [(E.sem, E.count) for E in self.engs.values() if E.count > 0]
        toks += [(s, v) for key, (s, v) in self.dma_sems.items() if v > 0 and (final or not str(key).startswith("wc"))]
        for E in self.engs.values():
            self._wait(E, toks)


def _small_layout():
    off = {}
    n = 0
    for name, w in (("g_mix", 16), ("g_ffn", 16), ("g_fin", 8), ("lamre", 8), ("lamim", 8),
                    ("logdt", 8), ("Bre", 128), ("Bim", 128), ("Cre", 128), ("Cim", 128),
                    ("Dcol", 2), ("bglu", 2), ("cw_ffn", 2 * 44 * 3), ("cb_ffn", 2 * 44),
                    ("cw_od", 24), ("cb_od", 8)):
        off[name] = n
        n += w
    return off, n


SOFF, NSMALL = _small_layout()
MOFF = {"gvg": 0, "wsT": 768, "wglu": 1536, "ident": 2048, "mask": 2176}
NMED = 2304


def build_program(nseg_total=NSEQ * NSEGS, stop_after=None):
    nc = bass.Bass("TRN2", target_bir_lowering=False)
    x_d = nc.dram_tensor("x", [NSEQ, SEQ, D], F32, kind="ExternalInput").ap()
    wsl_d = nc.dram_tensor("wsl", [NSLAB, 128, 2048], F32, kind="ExternalInput").ap()
    small_d = nc.dram_tensor("smallp", [128, NSMALL], F32, kind="ExternalInput").ap()
    med_d = nc.dram_tensor("medp", [128, NMED], F32, kind="ExternalInput").ap()
    bs_d = nc.dram_tensor("bsrow", [1, 768], F32, kind="ExternalInput").ap()
    out_d = nc.dram_tensor("out", [NSEQ, SEQ, D], F32, kind="ExternalOutput").ap()
    wbf_d = nc.dram_tensor("wbf_scratch", [NSLAB, 128, 2048], BF16, kind="Internal").ap()

    with ExitStack() as st:
        tr = Tracker(nc, st)

        uniq = [0]

        def sb(name, shape, dt, stack=st):
            uniq[0] += 1
            return stack.enter_context(nc.sbuf_tensor("%s_%d" % (name, uniq[0]), shape, dt))

        smallp = sb("smallp_s", [128, NSMALL], F32)
        Wst = sb("Wst", [128, 64, 128], BF16)
        Wout = sb("Wout", [128, 64, 128], BF16)
        Kbd = sb("Kbd", [128, 8, 128], BF16)
        aRe = sb("aRe", [128, 8, 8], F32)
        aIm = sb("aIm", [128, 8, 8], F32)
        naIm = sb("naIm", [128, 8, 8], F32)
        carry = sb("carry", [128, 8, 2], F32)
        gvg = sb("gvg", [128, 768], F32)
        WsT = sb("WsT", [128, 6, 128], BF16)
        wglu = sb("wglu", [128, 2, 256], BF16)
        ident = sb("ident", [128, 128], F32)
        onesM = sb("onesM", [128, 128], BF16)
        ones2 = sb("ones2", [33, 128], BF16)
        bs2 = sb("bs2", [33, 768], BF16)
        epsc = sb("epsc", [128, 1], F32)
        dummy = sb("dummy", [128, 2], F32)
        B_dummy = Buf("dummy")
        halo_f = sb("halo_f", [128, 2, 2, 44, 2], F32)
        halo_q = sb("halo_q", [128, 2, 8, 2], F32)
        B_small = Buf("small")
        B_wbf = [Buf("wbf%d" % i) for i in range(NSLAB)]
        ncast = [0]

        def emit_cast(li):
            assert li == ncast[0]
            tr.dma("wc%d" % (li % 3), lambda e: e.dma_start(out=wbf_d[li], in_=wsl_d[li]), writes=[B_wbf[li]], ename="pool")
            ncast[0] += 1
        B_tab = Buf("tab")
        B_carry = Buf("carry")
        B_halo_f = [[[Buf("hf%d_%d_%d" % (pp, l, m)) for m in range(44)] for l in range(2)] for pp in range(2)]
        B_halo_q = [[Buf("hq%d_%d" % (pp, j)) for j in range(8)] for pp in range(2)]

        ps_all = st.enter_context(nc.psum_tensor("ps_all", [128, 4096], F32))
        psum = [ps_all[:, i * 512:(i + 1) * 512] for i in range(8)]
        B_ps = [Buf("ps%d" % i) for i in range(8)]
        bank_ctr = [0]

        def bank():
            i = bank_ctr[0] % 8
            bank_ctr[0] += 1
            return psum[i], B_ps[i]

        def bank2():
            if bank_ctr[0] % 2:
                bank_ctr[0] += 1
            i = bank_ctr[0] % 8
            bank_ctr[0] += 2
            return ps_all[:, i * 512:(i + 2) * 512], [psum[i], psum[i + 1]], [B_ps[i], B_ps[i + 1]]

        st.enter_context(nc.Block())

        def sm(name, j=0, w=1):
            o = SOFF[name] + j
            return smallp[:, o:o + w]

        def mm(out_ap, lhsT, rhs, start, stop):
            return lambda e: e.matmul(out_ap, lhsT=lhsT, rhs=rhs, start=start, stop=stop)

        def mm_group(out_ap, pairs, reads, writes):
            n = len(pairs)
            fns = [mm(out_ap, l, r, i == 0, i == n - 1) for i, (l, r) in enumerate(pairs)]
            tr.group("pe", fns, reads=reads, writes=writes)

        def mm_group_y(out_ap, wfn, tt, reads, writes, perk=False):
            if not perk:
                mm_group(out_ap, [(wfn(k), y[:, k, tl(tt)]) for k in range(8)], reads=reads + [B_y[tt]], writes=writes)
                return
            for k in range(8):
                tr.group("pe", [mm(out_ap, wfn(k), y[:, k, tl(tt)], k == 0, k == 7)], reads=reads + Byj(k, tt), writes=writes)

        def stt(out, in0, scalar, in1, op0=ALU.mult, op1=ALU.add):
            return lambda e: e.scalar_tensor_tensor(out=out, in0=in0, scalar=scalar, in1=in1, op0=op0, op1=op1)

        def tt_(out, in0, in1, op):
            return lambda e: e.tensor_tensor(out=out, in0=in0, in1=in1, op=op)

        def ts_(out, in0, s1, s2, op0, op1=None):
            if op1 is None:
                return lambda e: e.tensor_scalar(out=out, in0=in0, scalar1=s1, scalar2=None, op0=op0)
            return lambda e: e.tensor_scalar(out=out, in0=in0, scalar1=s1, scalar2=s2, op0=op0, op1=op1)

        def act_(out, in_, func, bias=None, scale=None):
            kw = {}
            if bias is not None:
                kw["bias"] = bias
            if scale is not None:
                kw["scale"] = scale
            return lambda e: e.activation(out=out, in_=in_, func=func, **kw)

        def cp_(out, in_):
            return lambda e: e.tensor_copy(out=out, in_=in_)

        def acp_(out, in_):
            return lambda e: e.copy(out=out, in_=in_)

        tr.dma("small", lambda e: e.dma_start(out=smallp[:], in_=small_d[:, :]), writes=[B_small])

        tr.op("pool", lambda e: e.memset(onesM[:], 1.0 / 1024.0), writes=[B_tab])
        tr.op("pool", lambda e: e.memset(ones2[:], 0.0), writes=[B_tab])
        tr.op("pool", lambda e: e.memset(bs2[:], 0.0), writes=[B_tab])
        tr.op("pool", lambda e: e.memset(ones2[0:1, :], 1.0), writes=[B_tab])
        tr.op("pool", lambda e: e.memset(ones2[32:33, :], 1.0), writes=[B_tab])
        tr.op("pool", lambda e: e.memset(epsc[:], EPS), writes=[B_tab])
        tr.op("pool", lambda e: e.memset(Wout[:], 0.0), writes=[B_tab])

        with ExitStack() as ps:
            medp = sb("medp_s", [128, NMED], F32, ps)
            B_med = Buf("med")
            bsf = sb("bsf", [33, 768], F32, ps)
            bsh = sb("bsh", [33, 768], F32, ps)
            B_bsf = Buf("bsf")
            tr.dma("bsf0", lambda e: e.dma_start(out=bsf[0:1, :], in_=bs_d[:, :]), writes=[B_bsf])
            tr.dma("bsf1", lambda e: e.dma_start(out=bsf[32:33, :], in_=bs_d[:, :]), writes=[B_bsf])
            tr.op("dve", cp_(bs2[0:1, :], bsf[0:1, :]), reads=[B_bsf], writes=[B_tab])
            tr.op("dve", cp_(bsh[32:33, :], bsf[32:33, :]), reads=[B_bsf], writes=[B_bsf])
            tr.op("dve", cp_(bs2[32:33, :], bsf[32:33, :]), reads=[B_bsf], writes=[B_tab])
            tr.op("dve", tt_(bsh[32:33, :], bsf[32:33, :], bs2[32:33, :], ALU.subtract), reads=[B_bsf, B_tab], writes=[B_bsf])
            tr.op("dve", cp_(bs2[32:33, :], bsh[32:33, :]), reads=[B_bsf], writes=[B_tab])
            tr.dma("med", lambda e: e.dma_start(out=medp[:], in_=med_d[:, :]), writes=[B_med])
            tr.op("act", acp_(gvg[:], medp[:, 0:768]), reads=[B_med], writes=[B_tab])
            tr.op("act", acp_(ident[:], medp[:, 2048:2176]), reads=[B_med], writes=[B_tab])
            tr.op("act", acp_(wglu[:].rearrange("p k c -> p (k c)"), medp[:, 1536:2048]), reads=[B_med], writes=[B_tab])
            for hd in range(6):
                tr.op("dve", tt_(WsT[:, hd, :], medp[:, 768 + hd * 128:768 + (hd + 1) * 128],
                                 medp[:, 2176:2304], ALU.mult), reads=[B_med], writes=[B_tab])

            def t8(name, w=8):
                return sb(name, [128, w], F32, ps)
            B_p = Buf("s5p")

            def P_(eng, fn):
                tr.op(eng, fn, reads=[B_p, B_small], writes=[B_p])

            lr = t8("lr"); dt_ = t8("dt"); lrdt = t8("lrdt"); th = t8("th"); mag = t8("mag")
            cs = t8("cs"); sn = t8("sn"); tmp = t8("tmp"); tmp2 = t8("tmp2"); ki = sb("ki", [128, 8], I32, ps)
            P_("dve", ts_(lr[:], sm("lamre", 0, 8), -1e-4, None, ALU.min))
            P_("act", act_(dt_[:], sm("logdt", 0, 8), AF.Exp))
            P_("dve", tt_(lrdt[:], lr[:], dt_[:], ALU.mult))
            P_("dve", tt_(th[:], sm("lamim", 0, 8), dt_[:], ALU.mult))
            P_("act", act_(mag[:], lrdt[:], AF.Exp))

            def sin_of(dst, shift):
                P_("dve", ts_(tmp[:], th[:], shift, 1.0 / (2 * np.pi), ALU.add, ALU.mult))
                P_("dve", cp_(ki[:], tmp[:]))
                P_("dve", cp_(tmp[:], ki[:]))
                P_("dve", ts_(tmp2[:], th[:], shift, None, ALU.add))
                P_("dve", stt(tmp2[:], tmp[:], -2 * np.pi, tmp2[:]))
                P_("dve", ts_(tmp[:], tmp2[:], float(np.pi), -2 * np.pi, ALU.is_gt, ALU.mult))
                P_("dve", tt_(tmp2[:], tmp2[:], tmp[:], ALU.add))
                P_("dve", ts_(tmp[:], tmp2[:], -float(np.pi), 2 * np.pi, ALU.is_lt, ALU.mult))
                P_("dve", tt_(tmp2[:], tmp2[:], tmp[:], ALU.add))
                P_("dve", ts_(tmp2[:], tmp2[:], 3.1415925, -3.1415925, ALU.min, ALU.max))
                P_("act", act_(dst[:], tmp2[:], AF.Sin))

            sin_of(sn, 0.0)
            sin_of(cs, float(np.pi / 2))

            An = sb("An", [128, 5, 2, 8], F32, ps)
            P_("pool", lambda e: e.memset(An[:, 0, 0, :], 1.0))
            P_("pool", lambda e: e.memset(An[:, 0, 1, :], 0.0))
            P_("dve", tt_(An[:, 1, 0, :], mag[:], cs[:], ALU.mult))
            P_("dve", tt_(An[:, 1, 1, :], mag[:], sn[:], ALU.mult))

            def cmul(ore, oim, are, aim, bre, bim):
                P_("dve", tt_(tmp[:], are, bre, ALU.mult))
                P_("dve", tt_(tmp2[:], aim, bim, ALU.mult))
                P_("dve", tt_(ore, tmp[:], tmp2[:], ALU.subtract))
                P_("dve", tt_(tmp[:], are, bim, ALU.mult))
                P_("dve", tt_(tmp2[:], aim, bre, ALU.mult))
                P_("dve", tt_(oim, tmp[:], tmp2[:], ALU.add))

            for n in range(2, 5):
                cmul(An[:, n, 0, :], An[:, n, 1, :], An[:, n - 1, 0, :], An[:, n - 1, 1, :], An[:, 1, 0, :], An[:, 1, 1, :])
            qre = t8("qre"); qim = t8("qim"); q2re = t8("q2re"); q2im = t8("q2im")
            P_("dve", cp_(qre[:], An[:, 4, 0, :]))
            P_("dve", cp_(qim[:], An[:, 4, 1, :]))
            for k in range(8):
                P_("dve", cp_(aRe[:, :, k], qre[:]))
                P_("dve", cp_(aIm[:, :, k], qim[:]))
                P_("dve", ts_(naIm[:, :, k], qim[:], -1.0, None, ALU.mult))
                if k < 7:
                    cmul(q2re[:], q2im[:], qre[:], qim[:], qre[:], qim[:])
                    P_("dve", cp_(qre[:], q2re[:]))
                    P_("dve", cp_(qim[:], q2im[:]))
            den = t8("den"); zre = t8("zre"); zim = t8("zim"); nr = t8("nr")
            P_("dve", tt_(den[:], lr[:], lr[:], ALU.mult))
            P_("dve", tt_(tmp[:], sm("lamim", 0, 8), sm("lamim", 0, 8), ALU.mult))
            P_("dve", tt_(den[:], den[:], tmp[:], ALU.add))
            P_("dve", lambda e: e.reciprocal(out=den[:], in_=den[:]))
            P_("dve", ts_(nr[:], An[:, 1, 0, :], -1.0, None, ALU.add))
            P_("dve", tt_(tmp[:], nr[:], lr[:], ALU.mult))
            P_("dve", tt_(tmp2[:], An[:, 1, 1, :], sm("lamim", 0, 8), ALU.mult))
            P_("dve", tt_(zre[:], tmp[:], tmp2[:], ALU.add))
            P_("dve", tt_(zre[:], zre[:], den[:], ALU.mult))
            P_("dve", tt_(tmp[:], An[:, 1, 1, :], lr[:], ALU.mult))
            P_("dve", tt_(tmp2[:], nr[:], sm("lamim", 0, 8), ALU.mult))
            P_("dve", tt_(zim[:], tmp[:], tmp2[:], ALU.subtract))
            P_("dve", tt_(zim[:], zim[:], den[:], ALU.mult))

            def v3(name, a, b):
                return sb(name, [128, a, b], F32, ps)
            Bbre = v3("Bbre", 8, 16); Bbim = v3("Bbim", 8, 16); t3a = v3("t3a", 8, 16); t3b = v3("t3b", 8, 16)
            Bre3 = sm("Bre", 0, 128).rearrange("p (j h) -> p j h", j=8)
            Bim3 = sm("Bim", 0, 128).rearrange("p (j h) -> p j h", j=8)
            Cre3 = sm("Cre", 0, 128).rearrange("p (j h) -> p j h", j=8)
            Cim3 = sm("Cim", 0, 128).rearrange("p (j h) -> p j h", j=8)

            def bc(ap2):
                return ap2.unsqueeze(2).broadcast_to([128, 8, 16])

            def cmul3(ore, oim, sre, sim, vre, vim, neg_im=False):
                P_("dve", tt_(t3a[:], vre, bc(sre), ALU.mult))
                P_("dve", tt_(t3b[:], vim, bc(sim), ALU.mult))
                P_("dve", tt_(ore, t3a[:], t3b[:], ALU.subtract))
                P_("dve", tt_(t3a[:], vim, bc(sre), ALU.mult))
                P_("dve", tt_(t3b[:], vre, bc(sim), ALU.mult))
                if neg_im:
                    P_("dve", tt_(t3a[:], t3a[:], t3b[:], ALU.add))
                    P_("dve", ts_(oim, t3a[:], -1.0, None, ALU.mult))
                else:
                    P_("dve", tt_(oim, t3a[:], t3b[:], ALU.add))

            cmul3(Bbre[:], Bbim[:], zre[:], zim[:], Bre3, Bim3)

            Tpad = sb("Tpad", [128, 8, 8, 128], F32, ps)
            CApad = sb("CApad", [128, 8, 8, 128], F32, ps)
            Tn = v3("Tn_re", 8, 16); Tni = v3("Tn_im", 8, 16)
            P_("pool", lambda e: e.memset(Tpad[:], 0.0))
            P_("pool", lambda e: e.memset(CApad[:], 0.0))
            for li in range(16):
                emit_cast(li)

            B_scat = []

            def scatter(dst4, sr, src_re, src_im):
                for g2 in range(2):
                    pr = slice(64 * g2, 64 * g2 + 64)
                    for jl in range(4):
                        c0 = jl * 32 + 16 * g2
                        for ri, src in ((0, src_re), (1, src_im)):
                            bsc = Buf("sc")
                            B_scat.append(bsc)
                            tr.op("dve", cp_(dst4[pr, jl::4, sr * 2 + ri, c0:c0 + 16], src[pr, jl::4, :]), reads=[B_p], writes=[bsc])

            for s in range(4):
                n = 3 - s
                cmul3(Tn[:], Tni[:], An[:, n, 0, :], An[:, n, 1, :], Bbre[:], Bbim[:])
                scatter(Tpad, s, Tn, Tni)
            for n in range(5):
                cmul3(Tn[:], Tni[:], An[:, n, 0, :], An[:, n, 1, :], Cre3, Cim3, neg_im=True)
                if n < 4:
                    scatter(CApad, n, Tn, Tni)
                if n >= 1:
                    t = n - 1
                    Wout4 = Wout[:].rearrange("p (j t r) c -> p j (t r) c", j=8, t=4, r=2)
                    for g2 in range(2):
                        pr = slice(64 * g2, 64 * g2 + 64)
                        for jl in range(4):
                            c0 = jl * 32 + 16 * g2
                            for ri, src in ((0, Tn), (1, Tni)):
                                tr.op("dve", cp_(Wout4[pr, jl::4, t * 2 + ri, c0:c0 + 16], src[pr, jl::4, :]),
                                      reads=[B_p], writes=[Buf("scw")])
            for q in range(16):
                Pb, Bb = bank()
                fns = []
                for i in range(4):
                    idx = q * 4 + i
                    fns.append(lambda e, idx=idx, i=i, Pb=Pb: e.transpose(
                        out=Pb[:, i * 128:(i + 1) * 128], in_=Tpad[:, idx // 8, idx % 8, :], identity=ident[:]))
                tr.group("pe", fns, reads=[B_p, B_tab] + B_scat, writes=[Bb])
                tr.op("act", acp_(Wst[:, q * 4:(q + 1) * 4, :], Pb[:].rearrange("p (a c) -> p a c", a=4)),
                      reads=[Bb], writes=[B_tab])
            for kt in range(2):
                Pb, Bb = bank()
                fns = []
                for tau in range(4):
                    lst = [(jp, ri) for jp in range(kt * 4, kt * 4 + 4) for ri in range(2)]
                    for i, (jp, ri) in enumerate(lst):
                        fns.append(mm(Pb[:, tau * 128:(tau + 1) * 128], Tpad[:, jp, 6 + ri, :], CApad[:, jp, tau * 2 + ri, :],
                                      i == 0, i == len(lst) - 1))
                tr.group("pe", fns, reads=[B_p] + B_scat, writes=[Bb])
                tr.op("dve", stt(Kbd[:, kt * 4, :], ident[:], sm("Dcol", kt, 1), Pb[:, 0:128]),
                      reads=[Bb, B_tab, B_small], writes=[B_tab])
                tr.op("act", acp_(Kbd[:, kt * 4 + 1:kt * 4 + 4, :], Pb[:, 128:512].rearrange("p (a c) -> p a c", a=3)),
                      reads=[Bb], writes=[B_tab])
            tr.barrier()
        h = sb("h", [128, 8, SEGT], F32)
        yraw = sb("yraw", [128, 8 * SEGT], BF16)
        y = yraw[:].rearrange("p (k t) -> p k t", k=8)
        ring = [sb("ring%d" % i, [128, 2048], BF16) for i in range(5)]
        B_hb = [Buf("hb%d" % i) for i in range(8)]
        B_h = [B_hb[0:4], B_hb[4:8]]
        B_yjb = [[Buf("y%d_%d" % (j, i)) for i in range(8)] for j in range(8)]
        B_y = [[B_yjb[j][tb] for j in range(8) for tb in range(4 * tt, 4 * tt + 4)] for tt in range(2)]

        def Byj(j, tt):
            return B_yjb[j][4 * tt:4 * tt + 4]
        B_ring = [Buf("ring%d" % i) for i in range(5)]

        def tl(tt):
            return slice(tt * TT, (tt + 1) * TT)

        NRING = 5

        class WS:
            def __init__(self):
                self.nprep = 0
                self.nring = 0
                self.dests = {}
                self.ringslot = {}

            def ready(self, n):
                return True

            def _prepare(self, n):
                li = n % NSLAB
                while ncast[0] < min(NSLAB, n + 17):
                    emit_cast(ncast[0])
                if n in self.dests:
                    dap, db, nel = self.dests[n]
                    k = dap.shape[1]
                    wv = wbf_d[li][:, 0:nel].rearrange("p (k c) -> p k c", k=k)
                    key = "wdst"
                else:
                    r = self.nring % NRING
                    self.nring += 1
                    self.ringslot[n] = r
                    dap, db, nel, k = ring[r][:], B_ring[r], 2048, None
                    wv = wbf_d[li]
                    key = "ring%d" % r
                if False:
                    s_ = n % 2
                    tr.dma("stg%d" % s_, lambda e: e.dma_start(out=stg[s_][:], in_=wsl_d[li]), writes=[B_stg[s_]])
                    src = stg[s_][:, 0:nel]
                    if k is not None:
                        src = src.rearrange("p (k c) -> p k c", k=k)
                    tr.op("act", acp_(dap, src), reads=[B_stg[s_]], writes=[db])
                    if nseg_total > 1:
                        tr.dma("wst%d" % s_, lambda e: e.dma_start(out=wv, in_=dap), reads=[db], writes=[B_wbf[li]])
                else:
                    tr.dma(key, lambda e: e.dma_start(out=dap, in_=wv), reads=[B_wbf[li]], writes=[db])

            def advance(self, i, total):
                while self.nprep <= i + 3 and self.nprep < total and self.ready(self.nprep):
                    self._prepare(self.nprep)
                    self.nprep += 1
                assert self.nprep > i, (self.nprep, i)

        ws = WS()
        ws_total = nseg_total * NSLAB
        ws_pos = [0]

        def next_slab(dest=None):
            i = ws_pos[0]
            ws_pos[0] += 1
            if dest is not None:
                ws.dests[i] = dest
            return i

        def use_slab(i):
            ws.advance(i, ws_total)
            if i in ws.dests:
                return None, None
            r = ws.ringslot[i]
            return ring[r], B_ring[r]

        def norm_stats(tt, out=None, Bout=None):
            R, BR = bank()
            mm_group(R[:, :], [(onesM[:], y[:, j, tl(tt)]) for j in range(8)], reads=[B_y[tt], B_tab], writes=[BR])
            tr.op("act", act_(R[:, :], R[:, :], AF.Ln, bias=epsc[:, 0:1], scale=1.0), reads=[BR, B_tab], writes=[BR])
            if out is not None:
                tr.op("act", act_(out, R[:, :], AF.Exp, scale=-0.5), reads=[BR], writes=[Bout])
                return R, BR
            tr.op("act", act_(R[:, :], R[:, :], AF.Exp, scale=-0.5), reads=[BR], writes=[BR])
            return R, BR

        def norm(gname, l):
            RR = [norm_stats(tt) for tt in range(2)] if NORM_IL else None
            for tt in range(2):
                R, BR = RR[tt] if NORM_IL else norm_stats(tt)
                for j in range(8):
                    tr.op("dve", stt(y[:, j, tl(tt)], h[:, j, tl(tt)], sm(gname, l * 8 + j, 1), R[:, :], ALU.mult, ALU.mult),
                          reads=[B_h[tt], BR, B_small], writes=Byj(j, tt))

        def resid_add(P, BP, m, tt):
            tr.op("dve", tt_(h[:, m, tl(tt)], P[:, :], h[:, m, tl(tt)], ALU.add), reads=[BP, B_h[tt]], writes=[B_h[tt]])
            tr.op("act", act_(y[:, m, tl(tt)], h[:, m, tl(tt)], AF.Square), reads=[B_h[tt]], writes=Byj(m, tt))

        def out_proj(mix, B_mix):
            tr.op("act", act_(dummy[:, 0:1], epsc[:, 0:1], AF.Ln), reads=[B_tab], writes=[B_dummy])
            for sl in range(4):
                i = next_slab()
                rg, Brg = use_slab(i)
                w3 = rg[:].rearrange("p (k c) -> p k c", k=8)
                for ml in range(2):
                    m = sl * 2 + ml
                    for tt in range(2):
                        P, BP = bank()
                        mm_group(P[:, :], [(w3[:, k, ml * 128:(ml + 1) * 128], mix[:, k, tl(tt)]) for k in range(8)],
                                 reads=[Brg, B_mix], writes=[BP])
                        resid_add(P, BP, m, tt)

        def conv_evac(P2, BP2, a, Ba, l, m, sp):
            cw2 = sm("cw_ffn", (l * 44 + m) * 3 + 2, 1)
            cb = sm("cb_ffn", l * 44 + m, 1)
            hout = halo_f[:, sp, l, m, :]
            tr.op("act", act_(a[:, :], P2[:, :], AF.Identity, bias=cb, scale=cw2), reads=BP2 + [B_small], writes=[Ba])
            tr.op("act", acp_(hout[:, :], P2[:, SEGT - 2:SEGT]), reads=BP2, writes=[B_halo_f[sp][l][m]])

        def conv_taps(P2, BP2, a, Ba, l, m):
            cw = lambda k: sm("cw_ffn", (l * 44 + m) * 3 + k, 1)
            tr.op("dve", stt(a[:, 1:SEGT], P2[:, 0:SEGT - 1], cw(1), a[:, 1:SEGT]), reads=BP2 + [Ba, B_small], writes=[Ba])
            tr.op("dve", stt(a[:, 2:SEGT], P2[:, 0:SEGT - 2], cw(0), a[:, 2:SEGT]), reads=BP2 + [Ba], writes=[Ba])

        def conv_corr(a, Ba, l, m, sp):
            cw = lambda k: sm("cw_ffn", (l * 44 + m) * 3 + k, 1)
            hin = halo_f[:, 1 - sp, l, m, :]
            Bhin = B_halo_f[1 - sp][l][m]
            tr.op("act", act_(a[:, 0:1], hin[:, 1:2], AF.Identity, bias=a[:, 0:1], scale=cw(1)), reads=[Bhin, Ba], writes=[Ba])
            tr.op("act", act_(a[:, 0:1], hin[:, 0:1], AF.Identity, bias=a[:, 0:1], scale=cw(0)), reads=[Bhin, Ba], writes=[Ba])
            tr.op("act", act_(a[:, 1:2], hin[:, 1:2], AF.Identity, bias=a[:, 1:2], scale=cw(0)), reads=[Bhin, Ba], writes=[Ba])

        def ffn(l, sp):
            norm("g_ffn", l)
            with ExitStack() as fs:
                actb = sb("actb", [128, NPAIR, SEGT], BF16, fs)
                acc = [sb("acc%d" % i, [128, SEGT], F32, fs) for i in range(4)]
                B_acc = [Buf("acc%d" % i) for i in range(4)]
                B_act = [Buf("act%d" % i) for i in range(NPAIR)]
                prev = None

                def tail_act(i_, ag, Bag, av, Bav):
                    conv_corr(ag, Bag, l, i_, sp)
                    conv_corr(av, Bav, l, NPAIR + i_, sp)
                    tr.op("act", act_(ag[:, :], ag[:, :], AF.Silu), reads=[Bag], writes=[Bag])

                def tail_dve(i_, ag, Bag, av, Bav):
                    tr.op("dve", tt_(actb[:, i_, :], ag[:, :], av[:, :], ALU.mult), reads=[Bag, Bav], writes=[B_act[i_]])

                for i_ in range(NPAIR):
                    i = next_slab()
                    rg, Brg = use_slab(i)
                    w3 = rg[:].rearrange("p (k c) -> p k c", k=8)
                    G2, Gb, BG = bank2()
                    V2, Vb, BV = bank2()
                    if i_ == 0:
                        for tt in range(2):
                            mm_group_y(Gb[tt], lambda k: w3[:, k, 0:128], tt, [Brg], [BG[tt]], perk=True)
                            mm_group_y(Vb[tt], lambda k: w3[:, k, 128:256], tt, [Brg], [BV[tt]], perk=(tt == 1))
                    else:
                        for tt in range(2):
                            mm_group_y(Gb[tt], lambda k: w3[:, k, 0:128], tt, [Brg], [BG[tt]])
                        for tt in range(2):
                            mm_group_y(Vb[tt], lambda k: w3[:, k, 128:256], tt, [Brg], [BV[tt]])
                    par = i_ % 2
                    ag, Bag = acc[par * 2], B_acc[par * 2]
                    av, Bav = acc[par * 2 + 1], B_acc[par * 2 + 1]
                    conv_evac(G2, BG, ag, Bag, l, i_, sp)
                    conv_taps(G2, BG, ag, Bag, l, i_)
                    if prev is not None:
                        tail_act(*prev)
                    conv_evac(V2, BV, av, Bav, l, NPAIR + i_, sp)
                    if prev is not None:
                        tail_dve(*prev)
                    conv_taps(V2, BV, av, Bav, l, NPAIR + i_)
                    prev = (i_, ag, Bag, av, Bav)
                tail_act(*prev)
                tail_dve(*prev)
                for m in range(8):
                    i0 = next_slab()
                    i1 = next_slab()
                    r0, Br0 = use_slab(i0)
                    r1, Br1 = use_slab(i1)
                    w0 = r0[:, 0:1408].rearrange("p (k c) -> p k c", k=11)
                    w1 = r1[:, 0:1408].rearrange("p (k c) -> p k c", k=11)
                    if m == 0:
                        tr.op("act", act_(dummy[:, 0:1], epsc[:, 0:1], AF.Ln), reads=[B_tab], writes=[B_dummy])
                    PB = [bank() for tt in range(2)]
                    NA = 16
                    for part in range(2):
                        ks = range(0, NA) if part == 0 else range(NA, 22)
                        for tt in range(2):
                            P, BP = PB[tt]
                            fns = []
                            for k in ks:
                                wk = w0[:, k, :] if k < 11 else w1[:, k - 11, :]
                                fns.append(mm(P[:, :], wk, actb[:, k, tl(tt)], k == 0, k == 21))
                            tr.group("pe", fns, reads=[Br0, Br1] + [B_act[k] for k in ks], writes=[BP])
                    for tt in range(2):
                        P, BP = PB[tt]
                        resid_add(P, BP, m, tt)
                tr.barrier()

        def mixer0():
            norm("g_mix", 0)
            with ExitStack() as ms:
                mix = sb("mix", [128, 8, SEGT], BF16, ms)
                ubf = sb("ubf", [128, 2, SEGT], BF16, ms)
                R12 = sb("R12", [128, 3400], F32, ms)
                Sprev = sb("Sprev", [128, 8, 2, NB], BF16, ms)
                vg = sb("vg", [128, 8, 768], BF16, ms)
                gvt = [sb("gvt%d" % i, [128, 256], F32, ms) for i in range(2)]
                junk = sb("junk", [128, 256], BF16, ms)
                ssq3 = sb("ssq3", [128, 8, 3], F32, ms)
                rstdv = sb("rstdv", [128, 8], F32, ms)
                gu = [sb("gu%d" % i, [128, 512], F32, ms) for i in range(12)]
                SP_ = 32
                XG = R12[:, 0:1024].rearrange("p (q r b) -> p q r b", q=2, r=2)
                XH = R12[:, 1024:1536].rearrange("p (q r b) -> p q r b", q=2, r=2)
                XS = [R12[:, 1536 + i * 384:1536 + (i + 1) * 384].rearrange("p (q r b) -> p q r b", q=2, r=2) for i in range(2)]
                XT = R12[:, 2304:2824].rearrange("p (q r b) -> p q r b", q=2, r=2)
                XM = R12[:, 2824:3336].rearrange("p (q r b) -> p q r b", q=2, r=2)
                ysf = R12[:, 0:2048].rearrange("p (k t) -> p k t", k=2)
                ybf = R12[:, 2048:3072].bitcast(BF16).rearrange("p (k t) -> p k t", k=2)
                B_mix = Buf("mix"); B_ubf = Buf("ubf"); B_Sprev = Buf("Sprev"); B_ysf = Buf("ysf"); B_ybf = Buf("ybf")
                def qr(nm):
                    return [[Buf("%s_%d_%d" % (nm, q, r)) for r in range(2)] for q in range(2)]
                B_G = qr("G"); B_H = qr("H"); B_S = [qr("S0"), qr("S1")]; B_T = qr("T"); B_M = qr("M")
                allX = [b for grp in (B_G, B_H, B_S[0], B_S[1], B_T, B_M) for qq in grp for b in qq]
                B_vg = [Buf("vg%d" % n) for n in range(8)]
                B_gvt = [Buf("gvt0"), Buf("gvt1")]; B_junk = Buf("junk"); B_ssq = Buf("ssq"); B_gu = [Buf("gu%d" % i) for i in range(12)]

                def s5_gen():
                    si_ = next_slab()
                    rg, Brg = use_slab(si_)
                    w3 = rg[:].rearrange("p (k c) -> p k c", k=8)
                    for m in range(2):
                        for tt in range(2):
                            P, BP = bank()
                            mm_group_y(P[:, :], lambda k: w3[:, k, m * 128:(m + 1) * 128], tt, [Brg], [BP], perk=(m == 0))
                            tr.op("act", acp_(ubf[:, m, tl(tt)], P[:, :]), reads=[BP], writes=[B_ubf])
                    for i in range(2):
                        tr.op("dve", lambda e, i=i: e.memset(XS[i][:, :, :, 0:SP_], 0.0), writes=[B_S[i][q][r] for q in range(2) for r in range(2)])
                    tr.op("dve", lambda e: e.memset(XT[:, :, :, 0:1], 0.0), writes=[B_T[q][r] for q in range(2) for r in range(2)])
                    yield

                    def cma(jps, k, n, out, B_out, X, B_Xs, Ad, B_Ads, tmp=None, B_tmp=None):
                        t_ = tmp if tmp is not None else out
                        Bt = B_tmp if B_tmp is not None else B_out
                        for q, jp in enumerate(jps):
                            for r in range(2):
                                tr.op("dve", stt(t_(q, r), X(q, r), aRe[:, jp, k:k + 1], Ad(q, r)),
                                      reads=B_Xs(q, r) + B_Ads(q, r) + [B_tab], writes=Bt(q, r))
                        for q, jp in enumerate(jps):
                            tr.op("dve", stt(out(q, 0), X(q, 1), naIm[:, jp, k:k + 1], t_(q, 0)),
                                  reads=B_Xs(q, 1) + Bt(q, 0), writes=B_out(q, 0))
                            tr.op("dve", stt(out(q, 1), X(q, 0), aIm[:, jp, k:k + 1], t_(q, 1)),
                                  reads=B_Xs(q, 0) + Bt(q, 1), writes=B_out(q, 1))

                    for rnd in range(4):
                        jps = [rnd * 2, rnd * 2 + 1]
                        for q, jp in enumerate(jps):
                            kt = jp // 4
                            P, BP = bank()
                            fns = []
                            for ri in range(2):
                                for s_ in range(4):
                                    fns.append(mm(P[:, ri * NB:(ri + 1) * NB], Wst[:, jp * 8 + s_ * 2 + ri, :],
                                                  ubf[:, kt, s_::4], s_ == 0, s_ == 3))
                            tr.group("pe", fns, reads=[B_ubf, B_tab], writes=[BP])
                            tr.op("act", acp_(XG[:, q, :, :], P[:, :].rearrange("p (r b) -> p r b", r=2)),
                                  reads=[BP], writes=[B_G[q][0], B_G[q][1]])
                            x0r = XG[:, q, 0, 0:1]
                            x0i = XG[:, q, 1, 0:1]
                            cr = carry[:, jp, 0:1]
                            ci = carry[:, jp, 1:2]
                            tr.op("dve", stt(x0r, cr, aRe[:, jp, 0:1], x0r), reads=[B_carry, B_tab], writes=[B_G[q][0]])
                            tr.op("dve", stt(x0i, ci, aRe[:, jp, 0:1], x0i), reads=[B_carry, B_tab], writes=[B_G[q][1]])
                            tr.op("dve", stt(x0r, ci, naIm[:, jp, 0:1], x0r), reads=[B_carry], writes=[B_G[q][0]])
                            tr.op("dve", stt(x0i, cr, aIm[:, jp, 0:1], x0i), reads=[B_carry], writes=[B_G[q][1]])
                        yield
                        cma(jps, 0, 128, lambda q, r: XH[:, q, r, :], lambda q, r: [B_H[q][r]],
                            lambda q, r: XG[:, q, r, 0::2], lambda q, r: [B_G[q][r]],
                            lambda q, r: XG[:, q, r, 1::2], lambda q, r: [B_G[q][r]])
                        yield
                        cma(jps, 1, 64, lambda q, r: XS[0][:, q, r, SP_:SP_ + 64], lambda q, r: [B_S[0][q][r]],
                            lambda q, r: XH[:, q, r, 0::2], lambda q, r: [B_H[q][r]],
                            lambda q, r: XH[:, q, r, 1::2], lambda q, r: [B_H[q][r]])
                        yield
                        src, dst = 0, 1
                        for s_ in range(6):
                            d = 1 << s_
                            cma(jps, s_ + 2, 64,
                                lambda q, r, dst=dst: XS[dst][:, q, r, SP_:SP_ + 64], lambda q, r, dst=dst: [B_S[dst][q][r]],
                                lambda q, r, src=src, d=d: XS[src][:, q, r, SP_ - d:SP_ - d + 64], lambda q, r, src=src: [B_S[src][q][r]],
                                lambda q, r, src=src: XS[src][:, q, r, SP_:SP_ + 64], lambda q, r, src=src: [B_S[src][q][r]])
                            src, dst = dst, src
                            yield
                        cma(jps, 1, 64, lambda q, r: XT[:, q, r, 1:129:2], lambda q, r: [B_T[q][r]],
                            lambda q, r, src=src: XS[src][:, q, r, SP_ - 1:SP_ + 63], lambda q, r, src=src: [B_S[src][q][r]],
                            lambda q, r: XH[:, q, r, 0::2], lambda q, r: [B_H[q][r]])
                        for q, jp in enumerate(jps):
                            tr.op("act", acp_(XT[:, q, :, 2:130:2], XS[src][:, q, :, SP_:SP_ + 64]),
                                  reads=[B_S[src][q][0], B_S[src][q][1]], writes=[B_T[q][0], B_T[q][1]])
                        yield
                        cma(jps, 0, 128, lambda q, r: Sprev[:, jps[q], r, 1::2], lambda q, r: [B_Sprev],
                            lambda q, r: XT[:, q, r, 0:128], lambda q, r: [B_T[q][r]],
                            lambda q, r: XG[:, q, r, 0::2], lambda q, r: [B_G[q][r]],
                            tmp=lambda q, r: XM[:, q, r, :], B_tmp=lambda q, r: [B_M[q][r]])
                        for q, jp in enumerate(jps):
                            rb = [B_T[q][0], B_T[q][1]]
                            tr.op("act", acp_(Sprev[:, jp, :, 2::2], XT[:, q, :, 1:128]), reads=rb, writes=[B_Sprev])
                            tr.op("act", acp_(Sprev[:, jp, :, 0:1], carry[:, jp, :].unsqueeze(2)), reads=[B_carry], writes=[B_Sprev])
                            tr.op("act", acp_(carry[:, jp, :].unsqueeze(2), XS[src][:, q, :, SP_ + 63:SP_ + 64]),
                                  reads=[B_S[src][q][0], B_S[src][q][1]], writes=[B_carry])
                        yield
                    for kt in range(2):
                        for hb in range(2):
                            P, BP = bank()
                            fns = []
                            for t in range(4):
                                lst = []
                                for s_ in range(t + 1):
                                    lst.append((Kbd[:, kt * 4 + (t - s_), :], ubf[:, kt, hb * 512 + s_:hb * 512 + 512:4]))
                                for jl in range(4):
                                    jp = kt * 4 + jl
                                    for ri in range(2):
                                        lst.append((Wout[:, jp * 8 + t * 2 + ri, :], Sprev[:, jp, ri, hb * 128:(hb + 1) * 128]))
                                for i, (l_, r_) in enumerate(lst):
                                    fns.append(mm(P[:, t::4], l_, r_, i == 0, i == len(lst) - 1))
                            tr.group("pe", fns, reads=[B_ubf, B_Sprev, B_tab], writes=[BP])
                            tr.op("act", act_(ysf[:, kt, tl(hb)], P[:, :], AF.Gelu_apprx_tanh), reads=[BP], writes=[B_ysf] + allX)
                            tr.op("act", acp_(ybf[:, kt, tl(hb)], ysf[:, kt, tl(hb)]), reads=[B_ysf], writes=[B_ybf] + allX)
                            yield
                    Ps = []
                    for m in range(2):
                        for tt in range(2):
                            P, BP = bank()
                            mm_group(P[:, :], [(wglu[:, k, m * 128:(m + 1) * 128], ybf[:, k, tl(tt)]) for k in range(2)],
                                     reads=[B_ybf, B_tab], writes=[BP])
                            Ps.append((P, BP, m, tt))
                    for P, BP, m, tt in Ps:
                        tr.op("act", act_(P[:, :], P[:, :], AF.Sigmoid, bias=sm("bglu", m, 1), scale=1.0), reads=[BP, B_small], writes=[BP])
                    for P, BP, m, tt in Ps:
                        tr.op("dve", tt_(mix[:, m, tl(tt)], ysf[:, m, tl(tt)], P[:, :], ALU.mult), reads=[B_ysf, BP], writes=[B_mix])
                    yield

                def gm_gen():
                    for sl in range(3):
                        i = next_slab()
                        rg, Brg = use_slab(i)
                        w3 = rg[:].rearrange("p (k c) -> p k c", k=8)
                        for n in range(8):
                            tok = slice(n * 128, (n + 1) * 128)
                            tt = n // 4
                            P, BP = bank()
                            mm_group(P[:, 0:256], [(y[:, k, tok], w3[:, k, :]) for k in range(8)], reads=[B_y[tt], Brg], writes=[BP])
                            u = (sl * 8 + n) % 2
                            g_ = gvt[u]
                            tr.op("act", act_(g_[:, :], P[:, 0:256], AF.Gelu_apprx_tanh), reads=[BP], writes=[B_gvt[u]])
                            tr.op("act", lambda e, g_=g_, n=n, sl=sl: e.activation(out=junk[:, :], in_=g_[:, :], func=AF.Square,
                                                                                 accum_out=ssq3[:, n, sl:sl + 1]),
                                  reads=[B_gvt[u]], writes=[B_junk, B_ssq])
                            tr.op("dve", tt_(vg[:, n, sl * 256:(sl + 1) * 256], g_[:, :], gvg[:, sl * 256:(sl + 1) * 256], ALU.mult),
                                  reads=[B_gvt[u], B_tab], writes=[B_vg[n]])
                            yield
                    tr.op("dve", tt_(rstdv[:, :], ssq3[:, :, 0], ssq3[:, :, 1], ALU.add), reads=[B_ssq], writes=[B_ssq])
                    tr.op("dve", tt_(rstdv[:, :], rstdv[:, :], ssq3[:, :, 2], ALU.add), reads=[B_ssq], writes=[B_ssq])
                    tr.op("act", act_(rstdv[:, :], rstdv[:, :], AF.Ln, bias=epsc[:, 0:1], scale=1.0 / 768.0), reads=[B_ssq, B_tab], writes=[B_ssq])
                    tr.op("act", act_(rstdv[:, :], rstdv[:, :], AF.Exp, scale=-0.5), reads=[B_ssq], writes=[B_ssq])
                    for n in range(8):
                        tr.op("act", act_(vg[:, n, :], vg[:, n, :], AF.Copy, scale=rstdv[:, n:n + 1]), reads=[B_ssq, B_vg[n]], writes=[B_vg[n]])
                    yield
                    units = []
                    for sl in range(3):
                        i = next_slab()
                        rg, Brg = use_slab(i)
                        w3 = rg[:].rearrange("p (k c) -> p k c", k=8)
                        for hl in range(2):
                            hd = sl * 2 + hl
                            for tt in range(2):
                                u = len(units)
                                PU, BPU = bank()
                                mm_group(PU[:, :], [(w3[:, k, hl * 128:(hl + 1) * 128], y[:, k, tl(tt)]) for k in range(8)],
                                         reads=[Brg, B_y[tt]], writes=[BPU])
                                tr.op("act", act_(gu[u][:, :], PU[:, :], AF.Gelu_apprx_tanh), reads=[BPU], writes=[B_gu[u]])
                                units.append((hd, tt, u))
                                yield
                    for hd, tt, u in units:
                        PG, BPG = bank()
                        fns = []
                        for c4 in range(4):
                            n = tt * 4 + c4
                            o = PG[:, c4 * 128:(c4 + 1) * 128]
                            fns.append(mm(o, vg[:, n, hd * 128:(hd + 1) * 128], WsT[:, hd, :], True, False))
                            fns.append(mm(o, ones2[:, :], bs2[:, hd * 128:(hd + 1) * 128], False, True))
                        tr.group("pe", fns, reads=[B_vg[tt * 4 + c] for c in range(4)] + [B_tab, B_small], writes=[BPG])
                        tr.op("dve", tt_(mix[:, 2 + hd, tl(tt)], gu[u][:, :], PG[:, :], ALU.mult), reads=[B_gu[u], BPG], writes=[B_mix])
                        yield

                gens = [s5_gen(), gm_gen()]
                while gens:
                    for g in list(gens):
                        try:
                            next(g)
                        except StopIteration:
                            gens.remove(g)
                tr.barrier()
                out_proj(mix, B_mix)
                tr.barrier()

        def mixer1():
            norm("g_mix", 1)
            with ExitStack() as ms:
                mix = sb("mix1", [128, 8, SEGT], BF16, ms)
                B_mix = Buf("mix1")
                hxs = [sb("hxs%d" % i, [128, 512], F32, ms) for i in range(2)]
                qb = [sb("qb%d" % i, [128, 514], F32, ms) for i in range(2)]
                cc = [sb("cc%d" % i, [128, 512], F32, ms) for i in range(2)]
                B_hxs = [Buf("hxs0"), Buf("hxs1")]; B_qb = [Buf("qb0"), Buf("qb1")]; B_cc = [Buf("cc0"), Buf("cc1")]
                slabB = None
                pending1 = []

                def tail1(j, tt, PBg, BPBg):
                    q_, c_ = qb[tt], cc[tt]
                    cw = lambda k: sm("cw_od", j * 3 + k, 1)
                    cb = sm("cb_od", j, 1)
                    tr.op("act", act_(c_[:, :], q_[:, 2:514], AF.Identity, bias=cb, scale=cw(2)), reads=[B_qb[tt], B_small], writes=[B_cc[tt]])
                    tr.op("dve", stt(c_[:, :], q_[:, 1:513], cw(1), c_[:, :]), reads=[B_qb[tt], B_cc[tt]], writes=[B_cc[tt]])
                    tr.op("dve", stt(c_[:, :], q_[:, 0:512], cw(0), c_[:, :]), reads=[B_qb[tt], B_cc[tt]], writes=[B_cc[tt]])
                    tr.op("dve", tt_(mix[:, j, tl(tt)], c_[:, :], PBg[:, :], ALU.mult), reads=[B_cc[tt], BPBg], writes=[B_mix])

                for j in range(8):
                    iA = next_slab()
                    if j % 2 == 0:
                        iB = next_slab()
                    rA, BrA = use_slab(iA)
                    if j % 2 == 0:
                        rB, BrB = use_slab(iB)
                        slabB = (rB, BrB)
                    rB, BrB = slabB
                    wA = rA[:].rearrange("p (k c) -> p k c", k=8)
                    wB = rB[:].rearrange("p (k c) -> p k c", k=8)
                    bo = (j % 2) * 128
                    cw = lambda k: sm("cw_od", j * 3 + k, 1)
                    cb = sm("cb_od", j, 1)
                    for tt in range(2):
                        PC, BPC = bank()
                        mm_group_y(PC[:, :], lambda k: wA[:, k, 0:128], tt, [BrA], [BPC], perk=(j == 0))
                        PH, BPH = bank()
                        mm_group_y(PH[:, :], lambda k: wA[:, k, 128:256], tt, [BrA], [BPH], perk=(j == 0))
                        PBg, BPBg = bank()
                        mm_group_y(PBg[:, :], lambda k: wB[:, k, bo:bo + 128], tt, [BrB], [BPBg], perk=(j == 0))
                        hx_, q_, c_ = hxs[tt], qb[tt], cc[tt]
                        tr.op("act", acp_(hx_[:, :], PH[:, :]), reads=[BPH], writes=[B_hxs[tt]])
                        tr.op("dve", cp_(q_[:, 0:2], halo_q[:, 1 - tt, j, :]), reads=[B_halo_q[1 - tt][j]], writes=[B_qb[tt]])
                        tr.op("dve", tt_(q_[:, 2:514], PC[:, :], hx_[:, :], ALU.mult), reads=[BPC, B_hxs[tt]], writes=[B_qb[tt]])
                        tr.op("dve", cp_(halo_q[:, tt, j, :], q_[:, 512:514]), reads=[B_qb[tt]], writes=[B_halo_q[tt][j]])
                        pending1.append((j, tt, PBg, BPBg))
                        if len(pending1) > (1 if PIPE_L1 else 0):
                            tail1(*pending1.pop(0))
                while pending1:
                    tail1(*pending1.pop(0))
                out_proj(mix, B_mix)
                tr.barrier()

        def seg_pos(si):
            return si // NSEGS, (si % NSEGS) * SEGT

        def load_dma(si, tb, xb, Bx, key):
            sq, t0 = seg_pos(si)
            tr.dma(key, lambda e: e.dma_start(out=xb[:], in_=x_d[sq, t0 + tb * 128:t0 + (tb + 1) * 128, :]), writes=[Bx])

        def load_compute(tb, xb, Bx):
            tt = tb // 4
            for jh in range(2):
                P, BP = bank()
                fns = [(lambda e, jl=jl, P=P, xb=xb, jh=jh: e.transpose(out=P[:, jl * 128:(jl + 1) * 128],
                                                                        in_=xb[:, (jh * 4 + jl) * 128:(jh * 4 + jl + 1) * 128],
                                                                        identity=ident[:])) for jl in range(4)]
                tr.group("pe", fns, reads=[Bx, B_tab], writes=[BP])
                tr.op("act", acp_(h[:, jh * 4:(jh + 1) * 4, tb * 128:(tb + 1) * 128], P[:, :].rearrange("p (a c) -> p a c", a=4)),
                      reads=[BP], writes=[B_hb[tb]])
            tr.op("act", act_(y[:, :, tb * 128:(tb + 1) * 128], h[:, :, tb * 128:(tb + 1) * 128], AF.Square),
                  reads=[B_hb[tb]], writes=[B_yjb[j][tb] for j in range(8)])

        for si in range(nseg_total):
            sq, t0 = seg_pos(si)
            if si % NSEGS == 0:
                tr.op("dve", lambda e: e.memset(carry[:], 0.0), writes=[B_carry])
                tr.op("dve", lambda e: e.memset(halo_f[:], 0.0), writes=[b for pp in B_halo_f for l in pp for b in l])
                tr.op("dve", lambda e: e.memset(halo_q[:], 0.0), writes=[b for pp in B_halo_q for b in pp])
            if si == 0:
                with ExitStack() as ls:
                    xi = [sb("xi0_%d" % i, [128, 1024], F32, ls) for i in range(4)]
                    B_xi = [Buf("xi0_%d" % i) for i in range(4)]
                    for tb in range(4):
                        load_dma(0, tb, xi[tb], B_xi[tb], "xi%d" % tb)
                    for tb in range(8):
                        load_compute(tb, xi[tb % 4], B_xi[tb % 4])
                        if tb + 4 < 8:
                            load_dma(0, tb + 4, xi[tb % 4], B_xi[tb % 4], "xi%d" % (tb % 4))
                    tr.barrier()
            if stop_after != "load":
                mixer0()
            if stop_after not in ("load", "mix0"):
                ffn(0, si % 2)
            if stop_after not in ("load", "mix0", "ffn0"):
                mixer1()
            if stop_after is None:
                ffn(1, si % 2)
            nxt = si + 1 if si + 1 < nseg_total else None
            with ExitStack() as bs_:
                xi = [sb("xi_%d" % i, [128, 1024], F32, bs_) for i in range(4)]
                ot = [sb("ot_%d" % i, [128, 1024], F32, bs_) for i in range(2)]
                ofb = [sb("ofb_%d" % i, [128, 8, 128], F32, bs_) for i in range(2)]
                rsb = sb("rsb", [128, SEGT], F32, bs_)
                B_xi = [Buf("xi%d" % i) for i in range(4)]
                B_ot = [Buf("ot0"), Buf("ot1")]
                B_ofb = [Buf("ofb0"), Buf("ofb1")]
                B_rsb = [Buf("rsb0"), Buf("rsb1")]
                if nxt is not None:
                    for tb in range(4):
                        load_dma(nxt, tb, xi[tb], B_xi[tb], "xi%d" % tb)
                if stop_after is None:
                    for tt in range(2):
                        norm_stats(tt, out=rsb[:, tl(tt)], Bout=B_rsb[tt])
                for tb in range(8):
                    tt = tb // 4
                    ob = ofb[tb % 2]
                    Bo = B_ofb[tb % 2]
                    tok = slice(tb * 128, (tb + 1) * 128)
                    if stop_after is None:
                        for j in range(8):
                            tr.op("dve", stt(ob[:, j, :], h[:, j, tok], sm("g_fin", j, 1), rsb[:, tok], ALU.mult, ALU.mult),
                                  reads=[B_hb[tb], B_rsb[tt], B_small], writes=[Bo])
                    else:
                        tr.op("dve", cp_(ob[:, :, :], h[:, :, tok]), reads=[B_hb[tb]], writes=[Bo])
                    xb = ot[tb % 2]
                    Bx = B_ot[tb % 2]
                    for jh in range(2):
                        P, BP = bank()
                        fns = [(lambda e, jl=jl, P=P, ob=ob, jh=jh: e.transpose(out=P[:, jl * 128:(jl + 1) * 128],
                                                                                in_=ob[:, jh * 4 + jl, :], identity=ident[:])) for jl in range(4)]
                        tr.group("pe", fns, reads=[Bo, B_tab], writes=[BP])
                        tr.op("act", acp_(xb[:, jh * 512:(jh + 1) * 512], P[:, :]), reads=[BP], writes=[Bx])
                    tr.dma("ot%d" % (tb % 2), lambda e, xb=xb, tb=tb: e.dma_start(out=out_d[sq, t0 + tb * 128:t0 + (tb + 1) * 128, :], in_=xb[:]),
                           reads=[Bx])
                    if nxt is not None:
                        load_compute(tb, xi[tb % 4], B_xi[tb % 4])
                        if tb + 4 < 8:
                            load_dma(nxt, tb + 4, xi[tb % 4], B_xi[tb % 4], "xi%d" % (tb % 4))
                tr.barrier()
        tr.barrier(final=True)
    return nc


def _kslab(W, cols):
    sub = W[:, cols]
    return sub.reshape(8, 128, 256).transpose(1, 0, 2).reshape(128, 2048)


def _dslab(Wd, m, kh):
    sub = Wd[kh * 1408:(kh + 1) * 1408, m * 128:(m + 1) * 128].reshape(11, 128, 128).transpose(1, 0, 2).reshape(128, 1408)
    out = np.zeros((128, 2048), np.float32)
    out[:, :1408] = sub
    return out


def build_slabs(ev_w_in, ev_w_out, od_w_in, od_w_out, ffn_w_up, ffn_w_down):
    slabs = []
    ar = np.arange
    Win0 = ev_w_in[0]
    slabs.append(_kslab(Win0, ar(0, 256)))
    for c in range(3):
        slabs.append(_kslab(Win0, 1024 + c * 256 + ar(256)))
    for c in range(3):
        slabs.append(_kslab(Win0, 256 + c * 256 + ar(256)))
    for c in range(4):
        slabs.append(_kslab(ev_w_out[0], c * 256 + ar(256)))

    def ffn_slabs(l):
        for i in range(NPAIR):
            cols = np.concatenate([i * 128 + ar(128), DFF + i * 128 + ar(128)])
            slabs.append(_kslab(ffn_w_up[l], cols))
        for m in range(8):
            for kh in range(2):
                slabs.append(_dslab(ffn_w_down[l], m, kh))
    ffn_slabs(0)
    Win1 = od_w_in[0]
    for j in range(8):
        slabs.append(_kslab(Win1, np.concatenate([1024 + j * 128 + ar(128), 2048 + j * 128 + ar(128)])))
        if j % 2 == 0:
            slabs.append(_kslab(Win1, j * 128 + ar(256)))
    for c in range(4):
        slabs.append(_kslab(od_w_out[0], c * 256 + ar(256)))
    ffn_slabs(1)
    assert len(slabs) == NSLAB
    return np.ascontiguousarray(np.stack(slabs, 0).astype(np.float32))


def build_small(inp):
    sp = np.zeros((128, NSMALL), np.float32)

    def put(name, arr):
        arr = np.asarray(arr, np.float32)
        sp[:, SOFF[name]:SOFF[name] + arr.shape[1]] = arr

    put("g_mix", inp["mix_norm_g"].reshape(2, 8, 128).transpose(2, 0, 1).reshape(128, 16))
    put("g_ffn", inp["ffn_norm_g"].reshape(2, 8, 128).transpose(2, 0, 1).reshape(128, 16))
    put("g_fin", inp["final_norm_g"].reshape(8, 128).T)

    def gp(a):
        return a.reshape(8, 2, 64).transpose(1, 2, 0).reshape(128, 8)
    put("lamre", gp(inp["s5_lam_re"][0]))
    put("lamim", gp(inp["s5_lam_im"][0]))
    put("logdt", gp(np.repeat(inp["s5_log_dt"][0][:, None], 64, axis=1)))

    def gph(a):
        return a.reshape(8, 2, 64, 16).transpose(1, 2, 0, 3).reshape(128, 128)
    put("Bre", gph(inp["s5_b_re"][0]))
    put("Bim", gph(inp["s5_b_im"][0]))
    put("Cre", gph(inp["s5_c_re"][0].transpose(0, 2, 1)))
    put("Cim", gph(inp["s5_c_im"][0].transpose(0, 2, 1)))
    put("Dcol", inp["s5_d"][0].reshape(2, 128).T)
    put("bglu", inp["s5_b_glu"][0].reshape(2, 128).T)
    put("cw_ffn", inp["ffn_conv_w"].reshape(2, 3, 44, 128).transpose(3, 0, 2, 1).reshape(128, 264))
    put("cb_ffn", inp["ffn_conv_b"].reshape(2, 44, 128).transpose(2, 0, 1).reshape(128, 88))
    put("cw_od", inp["od_conv_w"][0].reshape(3, 8, 128).transpose(2, 1, 0).reshape(128, 24))
    put("cb_od", inp["od_conv_b"][0].reshape(8, 128).T)
    med = np.zeros((128, NMED), np.float32)
    med[:, 0:768] = np.repeat(inp["gm_v_g"][0][None, :], 128, axis=0)
    med[:, 768:1536] = inp["gm_w_s"][0].transpose(2, 0, 1).reshape(128, 768)
    med[:, 1536:2048] = inp["s5_w_glu"][0].reshape(2, 128, 256).transpose(1, 0, 2).reshape(128, 512)
    med[:, 2048:2176] = np.eye(128, dtype=np.float32)
    med[:, 2176:2304] = np.triu(np.ones((128, 128), np.float32))
    bsrow = np.ascontiguousarray(inp["gm_b_s"][0].reshape(1, 768).astype(np.float32))
    return sp, med, bsrow


_NC_CACHE = {}


def kernel(**inputs):
    inp = {k: np.asarray(v) for k, v in inputs.items()}
    x = inp["x"].astype(np.float32, copy=False)
    wsl = build_slabs(inp["ev_w_in"], inp["ev_w_out"], inp["od_w_in"], inp["od_w_out"], inp["ffn_w_up"], inp["ffn_w_down"])
    sp, med, bsrow = build_small(inp)
    if "full" not in _NC_CACHE:
        _NC_CACHE["full"] = build_program()
    nc = _NC_CACHE["full"]
    in_maps = []
    for c in range(NCORES):
        in_maps.append({"x": np.ascontiguousarray(x[c * NSEQ:(c + 1) * NSEQ]), "wsl": wsl, "smallp": sp, "medp": med, "bsrow": bsrow})
    res = run_bass_kernel_spmd(nc, in_maps, core_ids=list(range(NCORES)))
    out = np.concatenate([np.asarray(r["out"]) for r in res.results], axis=0)
    return out.astype(np.float32, copy=False)
```
